# Optimizing a Trainium2 kernel written in Bass

```python
import jax, jax.numpy as jnp
from jax import lax
import numpy as np


D_MODEL = 2048
BATCH = 8
SEQ = 2048
DEPTH = 1
DEC_BATCH = 2
DEC_SEQ = 8192
PAST_LEN = 128

D_CONV = D_MODEL // 2
CONV_WIDTH = 31
N_HEADS = 8
QK_NOPE = 128
QK_ROPE = 64
V_DIM = 128
Q_LORA = 512
KV_LORA = 512
D_ATTN = N_HEADS * V_DIM
D_MIX = D_CONV + D_ATTN
D_IN = 2 * D_CONV + Q_LORA + KV_LORA + QK_ROPE
D_FF = -(-8 * D_MODEL // (3 * 256)) * 256
Q_BLOCK = 128
ROPE_BASE = 10000.0
EPS = 1e-6

kernel_name = 'hybrid_conv_mla_adaln_encoder'


def _rmsnorm(x, g):
    xf = x.astype(jnp.float32)
    y = xf * lax.rsqrt(jnp.mean(xf * xf, axis=-1, keepdims=True) + EPS)
    return (y * g.astype(jnp.float32)).astype(x.dtype)


def _layernorm(x, g, b):
    xf = x.astype(jnp.float32)
    mu = jnp.mean(xf, axis=-1, keepdims=True)
    xc = xf - mu
    y = xc * lax.rsqrt(jnp.mean(xc * xc, axis=-1, keepdims=True) + EPS)
    return (y * g.astype(jnp.float32) + b.astype(jnp.float32)).astype(x.dtype)


def _rope_tables(seq):
    inv = 1.0 / (ROPE_BASE ** (jnp.arange(0, QK_ROPE, 2, dtype=jnp.float32) / QK_ROPE))
    ang = jnp.arange(seq, dtype=jnp.float32)[:, None] * inv[None, :]
    return jnp.cos(ang), jnp.sin(ang)


def _rotate(x, cos, sin):
    x1, x2 = jnp.split(x.astype(jnp.float32), 2, axis=-1)
    return jnp.concatenate([x1 * cos - x2 * sin, x1 * sin + x2 * cos], axis=-1).astype(x.dtype)


def _conv_group(u, w_dw, b_dw, g_ln, b_ln):
    a, gate = jnp.split(u, 2, axis=-1)
    h = a * jax.nn.sigmoid(gate)
    h = lax.conv_general_dilated(
        h, w_dw[:, None, :], window_strides=(1,),
        padding=[(CONV_WIDTH // 2, CONV_WIDTH // 2)],
        dimension_numbers=('NWC', 'WIO', 'NWC'),
        feature_group_count=D_CONV) + b_dw
    return jax.nn.silu(_layernorm(h, g_ln, b_ln))


def _mla_attention(q_nope, q_rope, k_nope, k_rope, v):
    b, s, h, _ = q_nope.shape
    nb = s // Q_BLOCK
    scale = (QK_NOPE + QK_ROPE) ** -0.5

    def blocks(t):
        return jnp.moveaxis(t.reshape(b, nb, Q_BLOCK, *t.shape[2:]), 1, 0)

    def one_block(qs):
        qn, qr = qs
        sc = (jnp.einsum('bqhd,bkhd->bhqk', qn, k_nope, preferred_element_type=jnp.float32)
              + jnp.einsum('bqhr,bkr->bhqk', qr, k_rope, preferred_element_type=jnp.float32))
        p = jax.nn.softmax(sc * scale, axis=-1).astype(v.dtype)
        return jnp.einsum('bhqk,bkhd->bqhd', p, v)

    out = lax.map(one_block, (blocks(q_nope), blocks(q_rope)))
    return jnp.moveaxis(out, 0, 1).reshape(b, s, h * V_DIM)


def _layer(x, c, w_ada, b_ada, g_pre_mix, g_post_mix, w_in, w_dw, b_dw, g_conv, b_conv,
           g_q_lat, w_uq, g_kv_lat, w_ukv, w_out, g_pre_ffn, g_post_ffn, w_gate_up, w_down):
    b, s, _ = x.shape
    mod = jax.nn.silu(c) @ w_ada + b_ada
    sh_m, sc_m, gt_m, sh_f, sc_f, gt_f = jnp.split(mod[:, None, :], 6, axis=-1)

    h = _rmsnorm(x, g_pre_mix) * (1 + sc_m) + sh_m
    u = h @ w_in
    u_conv, q_lat, kv_lat, k_rope = jnp.split(
        u, [2 * D_CONV, 2 * D_CONV + Q_LORA, 2 * D_CONV + Q_LORA + KV_LORA], axis=-1)

    conv_out = _conv_group(u_conv, w_dw, b_dw, g_conv, b_conv)

    q = (_rmsnorm(q_lat, g_q_lat) @ w_uq).reshape(b, s, N_HEADS, QK_NOPE + QK_ROPE)
    q_nope, q_rope = jnp.split(q, [QK_NOPE], axis=-1)
    kv = (_rmsnorm(kv_lat, g_kv_lat) @ w_ukv).reshape(b, s, N_HEADS, QK_NOPE + V_DIM)
    k_nope, v = jnp.split(kv, [QK_NOPE], axis=-1)
    cos, sin = _rope_tables(s)
    q_rope = _rotate(q_rope, cos[None, :, None, :], sin[None, :, None, :])
    k_rope = _rotate(k_rope, cos[None], sin[None])
    attn_out = _mla_attention(q_nope, q_rope, k_nope, k_rope, v)

    mix = jnp.concatenate([conv_out, attn_out], axis=-1) @ w_out
    x = x + gt_m * _rmsnorm(mix, g_post_mix)

    h = _rmsnorm(x, g_pre_ffn) * (1 + sc_f) + sh_f
    gate, up = jnp.split(h @ w_gate_up, 2, axis=-1)
    f = (jax.nn.silu(gate) * up) @ w_down
    x = x + gt_f * _rmsnorm(f, g_post_ffn)
    return x


def setup_inputs(seed: int = 0) -> dict:
    key = jax.random.key(seed)
    ks = jax.random.split(key, 24)
    nrm = jax.random.normal
    f32 = jnp.float32

    def gain(k, n):
        return 1.0 + 0.02 * nrm(k, (DEPTH, n), f32)

    return {
        'x_prompt': nrm(ks[0], (BATCH, SEQ, D_MODEL), f32),
        'x_sample': nrm(ks[1], (DEC_BATCH, DEC_SEQ, D_MODEL), f32),
        'c_prompt': nrm(ks[2], (BATCH, D_MODEL), f32),
        'c_sample': nrm(ks[3], (DEC_BATCH, D_MODEL), f32),
        'w_ada': 0.5 * nrm(ks[4], (DEPTH, D_MODEL, 6 * D_MODEL), f32) * D_MODEL ** -0.5,
        'b_ada': 0.02 * nrm(ks[5], (DEPTH, 6 * D_MODEL), f32),
        'g_pre_mix': gain(ks[6], D_MODEL),
        'g_post_mix': gain(ks[7], D_MODEL),
        'w_in': nrm(ks[8], (DEPTH, D_MODEL, D_IN), f32) * D_MODEL ** -0.5,
        'w_dw': nrm(ks[9], (DEPTH, CONV_WIDTH, D_CONV), f32) * CONV_WIDTH ** -0.5,
        'b_dw': 0.02 * nrm(ks[10], (DEPTH, D_CONV), f32),
        'g_conv': gain(ks[11], D_CONV),
        'b_conv': 0.02 * nrm(ks[12], (DEPTH, D_CONV), f32),
        'g_q_lat': gain(ks[13], Q_LORA),
        'w_uq': nrm(ks[14], (DEPTH, Q_LORA, N_HEADS * (QK_NOPE + QK_ROPE)), f32) * Q_LORA ** -0.5,
        'g_kv_lat': gain(ks[15], KV_LORA),
        'w_ukv': nrm(ks[16], (DEPTH, KV_LORA, N_HEADS * (QK_NOPE + V_DIM)), f32) * KV_LORA ** -0.5,
        'w_out': nrm(ks[17], (DEPTH, D_MIX, D_MODEL), f32) * D_MIX ** -0.5,
        'g_pre_ffn': gain(ks[18], D_MODEL),
        'g_post_ffn': gain(ks[19], D_MODEL),
        'w_gate_up': nrm(ks[20], (DEPTH, D_MODEL, 2 * D_FF), f32) * D_MODEL ** -0.5,
        'w_down': nrm(ks[21], (DEPTH, D_FF, D_MODEL), f32) * D_FF ** -0.5,
    }


def reference(x_prompt, x_sample, c_prompt, c_sample, w_ada, b_ada, g_pre_mix, g_post_mix,
              w_in, w_dw, b_dw, g_conv, b_conv, g_q_lat, w_uq, g_kv_lat, w_ukv, w_out,
              g_pre_ffn, g_post_ffn, w_gate_up, w_down):
    y_prompt = x_prompt
    y_sample = x_sample
    for l in range(DEPTH):
        p = (w_ada[l], b_ada[l], g_pre_mix[l], g_post_mix[l], w_in[l], w_dw[l], b_dw[l],
             g_conv[l], b_conv[l], g_q_lat[l], w_uq[l], g_kv_lat[l], w_ukv[l], w_out[l],
             g_pre_ffn[l], g_post_ffn[l], w_gate_up[l], w_down[l])
        y_prompt = _layer(y_prompt, c_prompt, *p)
        y_sample = _layer(y_sample, c_sample, *p)
    return (y_prompt, y_sample)
```

```python
import numpy as np
from contextlib import ExitStack
import concourse.bass as bass
import concourse.mybir as mybir
from concourse.bass_utils import run_bass_kernel_spmd

F32 = mybir.dt.float32
BF16 = mybir.dt.bfloat16
AF = mybir.ActivationFunctionType
ALU = mybir.AluOpType

D = 2048
DC = 1024
NH = 8
DIN = 3136
DFF = 5632
NJ = DFF // 128
EPS = 1e-6
SCALE = 192 ** -0.5
ARENA_BYTES = 207 * 1024
G_END = 8 * 1024
ENGS = ('pe', 'act', 'dve', 'pool', 'sp')


class Buf:
    __slots__ = ('name', 'w', 'r', 'dcount', 'excl', 'bg')

    def __init__(self, name):
        self.name = name
        self.w = {}
        self.r = {}
        self.dcount = 0
        self.bg = False
        self.excl = False


def _merge(d, ev):
    s, v = ev
    if d.get(s, 0) < v:
        d[s] = v


class Prog:
    def __init__(self):
        self.ops = {e: [] for e in ENGS}
        self.cnt = {e: 0 for e in ENGS}
        self.seen = {e: {} for e in ENGS}
        self.bufs = {}
        self.dsems = {}

    def B(self, name):
        b = self.bufs.get(name)
        if b is None:
            b = Buf(name)
            self.bufs[name] = b
        return b

    def _waits(self, eng, evs):
        out = []
        seen = self.seen[eng]
        for s, v in evs.items():
            if seen.get(s, 0) >= v:
                continue
            seen[s] = v
            out.append((s, v))
        return out

    def _deps(self, reads, writes, pwrites):
        evs = {}
        for b in reads:
            for ev in b.w.items():
                _merge(evs, ev)
        for b in writes:
            for ev in b.w.items():
                _merge(evs, ev)
            for ev in b.r.items():
                _merge(evs, ev)
        for b in pwrites:
            for ev in b.r.items():
                _merge(evs, ev)
        return evs

    def _update(self, ev, reads, writes, pwrites):
        for b in reads:
            _merge(b.r, ev)
        for b in writes:
            b.w = {ev[0]: ev[1]}
            b.r = {}
        for b in pwrites:
            _merge(b.w, ev)

    def op(self, eng, fn, reads=(), writes=(), pwrites=()):
        if any(b.excl for b in reads):
            writes = list(writes) + [b for b in reads if b.excl]
            reads = [b for b in reads if not b.excl]
        evs = self._deps(reads, writes, pwrites)
        self.cnt[eng] += 1
        ev = ('E' + eng, self.cnt[eng])
        self.ops[eng].append((self._waits(eng, evs), fn, ev[0], 1))
        self._update(ev, reads, writes, pwrites)
        return ev

    def dma(self, q, fn, reads=(), writes=(), pwrites=(), owner=None):
        dst = owner if owner is not None else (writes[0] if writes else pwrites[0])
        evs = self._deps(reads, writes, pwrites)
        dst.dcount += 16
        sem = 'D' + dst.name
        self.dsems[sem] = dst
        ev = (sem, dst.dcount)
        self.ops[q].append((self._waits(q, evs), fn, sem, 16))
        self._update(ev, reads, writes, pwrites)
        return ev

    def barrier(self):
        evs = {}
        for e in ENGS:
            if self.cnt[e] > 0:
                evs['E' + e] = self.cnt[e]
        for sem, b in self.dsems.items():
            if not b.bg:
                evs[sem] = b.dcount
        for e in ENGS:
            self.ops[e].append((self._waits(e, evs), None, None, 0))
        for b in self.bufs.values():
            if not b.bg:
                b.w = {}
                b.r = {}

    def sem_names(self):
        names = ['E' + e for e in ENGS if self.cnt[e] > 0]
        names += list(self.dsems.keys())
        return names

    def replay(self, eng, e, sems):
        for waits, fn, sem, inc in self.ops[eng]:
            for s, v in waits:
                e.wait_ge(sems[s], v)
            if fn is not None:
                inst = fn(e)
                inst.then_inc(sems[sem], inc)


class Arena:
    def __init__(self, t, base=G_END):
        self.t = t
        self.off = base

    def alloc(self, dtype, shape):
        n = 1
        for s in shape[1:]:
            n *= s
        nbytes = n * (2 if dtype == BF16 else 4)
        nbytes = (nbytes + 63) // 64 * 64
        off = self.off
        self.off += nbytes
        assert self.off <= ARENA_BYTES, f"arena overflow {self.off}"
        v = self.t[:, off // 4:(off + nbytes) // 4]
        if dtype == BF16:
            v = v.bitcast(BF16)
        v = v[:, 0:n]
        if len(shape) == 3:
            v = v.rearrange("p (a b) -> p a b", a=shape[1])
        elif len(shape) == 4:
            v = v.rearrange("p (a b c) -> p a b c", a=shape[1], b=shape[2])
        return v


def build_nc(own_tiles=4, ctx_tiles=(4, 16), debug=False, phases='cs0cbA1A2BC1C2'):
    NOWN = own_tiles * 512
    NCTX = [ctx_tiles[0] * 512, ctx_tiles[1] * 512]
    nc = bass.Bass("TRN2", target_bir_lowering=False)

    def din(name, shape, dt=F32):
        return nc.dram_tensor(name, shape, dt, kind="ExternalInput").ap()

    def dscr(name, shape, dt):
        return nc.dram_tensor(name, shape, dt, kind=("ExternalOutput" if debug else "Internal")).ap()

    x_d = [din("xP", [NCTX[0], D]), din("xS", [NCTX[1], D])]
    halo_d = din("halo", [2, 32, D])
    hmask_d = din("hmask", [128, 2, 32])
    cpair_d = din("cpair", [2, D])
    rope_d = [din("ropeP", [2, 64, NCTX[0]]), din("ropeS", [2, 64, NCTX[1]])]
    ident_d = din("ident", [128, 128])
    w_ada = din("w_ada", [D, 6 * D])
    b_ada = din("b_ada", [6 * D])
    g_pre_mix = din("g_pre_mix", [D])
    g_post_mix = din("g_post_mix", [D])
    w_in = din("w_in", [D, DIN])
    w_dw = din("w_dw", [31, DC])
    b_dw = din("b_dw", [DC])
    g_conv = din("g_conv", [DC])
    b_conv = din("b_conv", [DC])
    g_q_lat = din("g_q_lat", [512])
    w_uq = din("w_uq", [512, 1536])
    g_kv_lat = din("g_kv_lat", [512])
    w_ukv = din("w_ukv", [512, 2048])
    w_out = din("w_out", [D, D])
    g_pre_ffn = din("g_pre_ffn", [D])
    g_post_ffn = din("g_post_ffn", [D])
    w_gu = din("w_gate_up", [D, 2 * DFF])
    w_dn = din("w_down", [DFF, D])

    y_d = [nc.dram_tensor("yP", [NOWN, D], F32, kind="ExternalOutput").ap(),
           nc.dram_tensor("yS", [NOWN, D], F32, kind="ExternalOutput").ap()]

    wbf_in = dscr("wbf_in", [D, 3200], BF16)
    wbf_uq = dscr("wbf_uq", [512, 2048], BF16)
    wbf_ukv = dscr("wbf_ukv", [512, 2048], BF16)
    wbf_out = dscr("wbf_out", [D, D], BF16)
    wbf_gu = dscr("wbf_gu", [D, 2 * DFF], BF16)
    wbf_dn = dscr("wbf_dn", [DFF, D], BF16)
    modscr = dscr("modscr", [2, 6, D], F32)
    h_scr = [dscr(f"h_scr{g}", [128, 16, NOWN + 32], BF16) for g in range(2)]
    kvn_scr = [dscr(f"kvn_scr{g}", [128, 4, NCTX[g]], BF16) for g in range(2)]
    kr_scr = [dscr(f"kr_scr{g}", [64, NCTX[g]], BF16) for g in range(2)]
    qn_scr = [dscr(f"qn_scr{g}", [128, 4, NOWN], BF16) for g in range(2)]
    conv_scr = [dscr(f"conv_scr{g}", [128, 8, NOWN], BF16) for g in range(2)]
    attn_scr = [dscr(f"attn_scr{g}", [128, 8, NOWN], BF16) for g in range(2)]
    x1_scr = [dscr(f"x1_scr{g}", [NOWN, D], F32) for g in range(2)]
    h2_scr = [dscr(f"h2_scr{g}", [128, 16, NOWN], BF16) for g in range(2)]

    P = Prog()
    B = P.B

    with ExitStack() as es:
        arena = es.enter_context(nc.sbuf_tensor("arena", [128, ARENA_BYTES // 4], F32))
        psum = es.enter_context(nc.psum_tensor("psum", [128, 8, 512], F32))

        def bank(i):
            return psum[:, i, :]

        def ptview(i):
            return psum[:, 2 * i:2 * i + 2, :].rearrange("p a b -> p (a b)").bitcast(BF16).rearrange(
                "p (c n) -> p c n", c=16)

        PSB = [B(f"ps{i}") for i in range(8)]
        STQ = ['sp', 'pool']
        for b_ in PSB:
            b_.excl = True

        GA = Arena(arena, 0)
        identf = GA.alloc(F32, [128, 128])
        identb = GA.alloc(BF16, [128, 128])
        ones = GA.alloc(BF16, [128, 128])
        onesf = GA.alloc(F32, [128, 128])
        vecT = GA.alloc(F32, [128, 6, 128])
        hmask = GA.alloc(F32, [128, 2, 32])
        ssx = GA.alloc(F32, [128, 8])
        assert GA.off <= G_END

        def G1T(g, c): return vecT[:, g, 16 + c:17 + c]
        def SH1T(g, c): return vecT[:, g, c:c + 1]
        def G2T(g, c): return vecT[:, g, 64 + c:65 + c]
        def SH2T(g, c): return vecT[:, g, 48 + c:49 + c]
        def GQ(c): return vecT[:, 2, c:c + 1]
        def GKV(c): return vecT[:, 2, 4 + c:5 + c]
        def BDW(c): return vecT[:, 2, 8 + c:9 + c]
        def GCV(c): return vecT[:, 2, 16 + c:17 + c]
        def BCV(c): return vecT[:, 2, 24 + c:25 + c]
        def WDW(k, c):
            idx = k * 8 + c
            return vecT[:, 3 + idx // 128, idx % 128:idx % 128 + 1]

        def dma(q, out, in_, reads=(), writes=(), pwrites=(), owner=None, **kw):
            return P.dma(q, lambda e: e.dma_start(out=out, in_=in_, **kw), reads=reads, writes=writes,
                         pwrites=pwrites, owner=owner)

        def act(out, in_, func, reads=(), writes=(), pwrites=(), **kw):
            return P.op('act', lambda e: e.activation(out=out, in_=in_, func=func, **kw), reads=reads,
                        writes=writes, pwrites=pwrites)

        def dve(fn, reads=(), writes=(), pwrites=()):
            return P.op('dve', fn, reads=reads, writes=writes, pwrites=pwrites)

        def mm(out, pairs, reads, wbuf, start=True, stop=True):
            def fn(e):
                n = len(pairs)
                inst = None
                for i, (l, r) in enumerate(pairs):
                    inst = e.matmul(out, lhsT=l, rhs=r, start=(start and i == 0), stop=(stop and i == n - 1))
                return inst
            return P.op('pe', fn, reads=reads, writes=[wbuf])

        def setup():
            A = Arena(arena)
            vr = A.alloc(F32, [128, 6, 128])
            dma('sp', identf, ident_d[:, :], writes=[B('identf')])
            dma('sp', hmask, hmask_d[:, :, :], writes=[B('hmask')])
            dve(lambda e: e.tensor_copy(out=identb, in_=identf), reads=[B('identf')], writes=[B('identb')])
            dve(lambda e: e.memset(ones, 1.0), writes=[B('ones')])
            dve(lambda e: e.memset(onesf, 1.0), writes=[B('onesf')])
            rows = [(g_q_lat, 4), (g_kv_lat, 4), (b_dw, 8), (g_conv, 8), (b_conv, 8)]
            r0 = 0
            for src, n in rows:
                dma('sp', vr[r0:r0 + n, 2, :], src.rearrange("(c p) -> c p", p=128), pwrites=[B('vr2')])
                r0 += n
            wv = w_dw.rearrange("k (c p) -> (k c) p", p=128)
            dma('sp', vr[:, 3, :], wv[0:128, :], writes=[B('vr3')])
            dma('sp', vr[0:120, 4, :], wv[128:248, :], writes=[B('vr4')])
            dma('sp', vr[0:32, 5, :], cpair_d.rearrange("g (c p) -> (g c) p", p=128), writes=[B('vr5')])
            for i, n in ((2, 32), (3, 128), (4, 120), (5, 32)):
                mm(bank(i)[:, 0:n], [(vr[0:n, i, :], identf[0:n, 0:n])], [B(f'vr{i}'), B('identf')], PSB[i])
                dve(lambda e, i=i, n=n: e.tensor_copy(out=vecT[:, i, 0:n], in_=bank(i)[:, 0:n]),
                    reads=[PSB[i]], pwrites=[B('vecT')])
            return vr

        def convert(dst, src, nrows, ncols, name, rows_per=128):
            for r in range(0, nrows, rows_per):
                dma('pool', dst[r:r + rows_per, 0:ncols], src[r:r + rows_per, 0:ncols], pwrites=[B(name)])

        def convert_pre():
            for r in range(0, D, 512):
                dma('pool', wbf_in[r:r + 512, 2048:3136], w_in[r:r + 512, 2048:3136], pwrites=[B('wbf_inA')])
            dma('pool', wbf_in[:, 3136:3168], w_in[:, 3104:3136], pwrites=[B('wbf_inA')])
            dma('pool', wbf_in[:, 3168:3200], w_in[:, 3072:3104], pwrites=[B('wbf_inA')])

        bg_chunks = []

        def convert_bg_early():
            for n_ in ('wbf_inC', 'wbf_uq', 'wbf_ukv', 'wbf_out', 'wbf_gu', 'wbf_dn'):
                B(n_).bg = True
            for r in range(0, D, 256):
                dma('pool', wbf_in[r:r + 256, 0:2048], w_in[r:r + 256, 0:2048], pwrites=[B('wbf_inC')])
            convert(wbf_uq, w_uq, 512, 1536, 'wbf_uq', 256)
            src = w_uq.rearrange("r (h c) -> r h c", h=NH)
            dst = wbf_uq[:, 1536:2048].rearrange("r (h c) -> r h c", h=NH)
            dma('pool', dst[:, :, 0:32], src[:, :, 160:192], pwrites=[B('wbf_uq')])
            dma('pool', dst[:, :, 32:64], src[:, :, 128:160], pwrites=[B('wbf_uq')])
            convert(wbf_ukv, w_ukv, 512, 2048, 'wbf_ukv', 256)
            for r in range(0, D, 256):
                bg_chunks.append((wbf_out[r:r + 256, :], w_out[r:r + 256, :], 'wbf_out'))
            for r in range(0, D, 128):
                bg_chunks.append((wbf_gu[r:r + 128, :], w_gu[r:r + 128, :], 'wbf_gu'))
            for r in range(0, DFF, 256):
                bg_chunks.append((wbf_dn[r:r + 256, :], w_dn[r:r + 256, :], 'wbf_dn'))

        def emit_bg(n, clock):
            for _ in range(n):
                if not bg_chunks:
                    return
                dst, src, name = bg_chunks.pop(0)
                dma('pool', dst, src, reads=clock, pwrites=[B(name)])

        def phase0(vr):
            A = Arena(arena)
            A.off = G_END + 128 * 6 * 4 + 64
            sT = A.alloc(BF16, [128, 16, 2])
            wblk = [A.alloc(BF16, [128, 16, 512]) for _ in range(2)]
            wf = [A.alloc(F32, [128, 16, 512]) for _ in range(3)]
            brow = [A.alloc(F32, [128, 512]) for _ in range(4)]
            grow = [A.alloc(F32, [128, 512]) for _ in range(4)]
            mrow = [A.alloc(F32, [128, 512]) for _ in range(2)]
            for g in range(2):
                act(sT[:, :, g], vecT[:, 5, g * 16:(g + 1) * 16], AF.Silu, reads=[B('vecT')], pwrites=[B('sT')])
            gvecs = {1: g_pre_mix, 2: g_post_mix, 4: g_pre_ffn, 5: g_post_ffn}
            def loads(b):
                sl = b % 3
                k2 = b % 2
                sec = b // 4
                c0 = (b % 4) * 512
                dma('sp' if b % 2 == 0 else 'act', wf[sl],
                    w_ada[:, b * 512:(b + 1) * 512].rearrange("(c p) n -> p c n", p=128), writes=[B(f'wf{sl}')])
                k4 = b % 4
                dma('sp', brow[k4][0:2, :], b_ada[b * 512:(b + 1) * 512].partition_broadcast(2),
                    writes=[B(f'brow{k4}')])
                if sec in gvecs:
                    dma('sp', grow[k4][0:2, :], gvecs[sec][c0:c0 + 512].partition_broadcast(2),
                        writes=[B(f'grow{k4}')])
            loads(0)
            loads(1)
            for b in range(24):
                sl = b % 3
                k2 = b % 2
                sec = b // 4
                c0 = (b % 4) * 512
                dve(lambda e, sl=sl, k2=k2: e.tensor_copy(out=wblk[k2][:, 0:8, :], in_=wf[sl][:, 0:8, :]),
                    reads=[B(f'wf{sl}')], pwrites=[B(f'wblk{k2}')])
                act(wblk[k2][:, 8:16, :], wf[sl][:, 8:16, :], AF.Copy, reads=[B(f'wf{sl}')],
                    pwrites=[B(f'wblk{k2}')])
                ps = bank(k2)[0:2, :]
                mm(ps, [(sT[:, c, :], wblk[k2][:, c, :]) for c in range(16)], [B('sT'), B(f'wblk{k2}')], PSB[k2])
                m = mrow[k2][0:2, :]
                mb = B(f'mrow{k2}')
                k4 = b % 4
                br = brow[k4][0:2, :]
                gr = grow[k4][0:2, :]
                if sec in (1, 4):
                    dve(lambda e, m=m, ps=ps, br=br: e.scalar_tensor_tensor(out=m, in0=ps, scalar=1.0, in1=br,
                                                                             op0=ALU.add, op1=ALU.add),
                        reads=[PSB[k2], B(f'brow{k4}')], writes=[mb])
                else:
                    dve(lambda e, m=m, ps=ps, br=br: e.tensor_tensor(out=m, in0=ps, in1=br, op=ALU.add),
                        reads=[PSB[k2], B(f'brow{k4}')], writes=[mb])
                if sec in gvecs:
                    dve(lambda e, m=m, gr=gr: e.tensor_tensor(out=m, in0=m, in1=gr, op=ALU.mult),
                        reads=[B(f'grow{k4}')], writes=[mb])
                dma('pool', modscr[:, sec, c0:c0 + 512], m, reads=[mb], pwrites=[B('modscr')], owner=mb)
                if b + 2 < 24:
                    loads(b + 2)
            for g in range(2):
                dma('sp', vr[0:96, g, :], modscr[g].rearrange("s (c p) -> (s c) p", p=128), reads=[B('modscr')],
                    writes=[B(f'vr{g}')])
                mm(bank(2 + g)[:, 0:96], [(vr[0:96, g, :], identf[0:96, 0:96])], [B(f'vr{g}'), B('identf')],
                   PSB[2 + g])
                dve(lambda e, g=g: e.tensor_copy(out=vecT[:, g, 0:96], in_=bank(2 + g)[:, 0:96]),
                    reads=[PSB[2 + g]], pwrites=[B('vecT')])

        class NormT:
            def __init__(self, A, nx=3):
                self.X = [A.alloc(F32, [128, D]) for _ in range(nx)]
                self.junk = A.alloc(BF16, [128, D])
                self.xn = [A.alloc(BF16, [128, D]) for _ in range(3)]
                self.i = 0
                self.pending = None

            def stats_rstd(self, xs, xb, n, col):
                sc = ssx[0:n, col:col + 1]
                sb = B(f'ssx{col}')
                act(self.junk[0:n, :], xs[0:n, :], AF.Square, reads=[xb], writes=[sb], accum_out=sc)
                act(sc, sc, AF.Sqrt, reads=[], writes=[sb], scale=1.0 / D, bias=EPS)
                dve(lambda e: e.reciprocal(out=sc, in_=sc), writes=[sb])
                return sc, sb

            def norm_T(self, xs, xb, n, hTv, hTb, col0, GT_, SHT_, pti):
                i = self.i
                self.i += 1
                sc, sb = self.stats_rstd(xs, xb, n, i % 4)
                xn = self.xn[i % 3]
                xnb = B(f'xn{i % 3}')
                dve(lambda e: e.tensor_scalar(out=xn[0:n, :], in0=xs[0:n, :], scalar1=sc, scalar2=None,
                                              op0=ALU.mult), reads=[xb, sb], writes=[xnb])
                prev = self.pending
                self.pending = (xn, xnb, n, hTv, hTb, col0, GT_, SHT_, pti)
                if prev is not None:
                    self._stage2(*prev)

            def flush(self):
                if self.pending is not None:
                    prev = self.pending
                    self.pending = None
                    self._stage2(*prev)

            def _stage2(self, xn, xnb, n, hTv, hTb, col0, GT_, SHT_, pti):
                pt = ptview(pti)
                for hb_ in range(2):
                    pb = PSB[2 * pti + hb_]

                    def tr(e, hb_=hb_):
                        inst = None
                        for c in range(8 * hb_, 8 * hb_ + 8):
                            inst = e.transpose(out=pt[:, c, 0:n], in_=xn[0:n, c * 128:(c + 1) * 128],
                                               identity=identb[0:n, 0:n])
                        return inst
                    P.op('pe', tr, reads=[xnb, B('identb')], writes=[pb])
                for c in (0, 1, 2, 3, 4):
                    act(hTv[:, c, col0:col0 + n], pt[:, c, 0:n], AF.Identity, reads=[PSB[2 * pti], B('vecT')],
                        pwrites=[hTb], scale=GT_(c), bias=SHT_(c))
                for c in (8, 9, 10, 11, 12, 13, 14, 15, 5, 6, 7):
                    o = hTv[:, c, col0:col0 + n]
                    src = pt[:, c, 0:n]
                    pb = PSB[2 * pti + c // 8]
                    dve(lambda e, o=o, src=src, c=c: e.tensor_scalar(out=o, in0=src, scalar1=GT_(c),
                                                                     scalar2=SHT_(c), op0=ALU.mult, op1=ALU.add),
                        reads=[pb, B('vecT')], pwrites=[hTb])

        def phaseA1(g):
            A = Arena(arena)
            winA = A.alloc(BF16, [128, 16, 1152])
            NT = NormT(A)
            hT = [A.alloc(BF16, [128, 16, 512]) for _ in range(3)]
            lat32 = [A.alloc(F32, [128, 4, 512]) for _ in range(2)]
            sq = [A.alloc(BF16, [128, 4, 512]) for _ in range(2)]
            rst = [A.alloc(F32, [128, 512]) for _ in range(2)]
            nst = [A.alloc(BF16, [128, 4, 512]) for _ in range(2)]
            ctab = [A.alloc(F32, [128, 512]) for _ in range(2)]
            stab = [A.alloc(F32, [128, 512]) for _ in range(2)]
            r1 = A.alloc(F32, [128, 512])
            r2 = A.alloc(F32, [128, 512])
            krs = A.alloc(BF16, [128, 512])
            dma('sp', winA, wbf_in[:, 2048:3200].rearrange("(c p) n -> p c n", p=128), reads=[B('wbf_inA')],
                writes=[B('winA')])
            nt = ctx_tiles[g]
            st = {'lat': 0, 'mb': 0}

            def transposes(t):
                hs = t % 3
                for s in range(4):
                    xi = NT.i
                    xs = NT.X[xi % 3]
                    xb = B(f'X{xi % 3}')
                    r0 = (t * 4 + s) * 128
                    dma('sp', xs, x_d[g][r0:r0 + 128, :], writes=[xb])
                    NT.norm_T(xs, xb, 128, hT[hs], B(f'hT{hs}'), s * 128, lambda c: G1T(g, c),
                              lambda c: SH1T(g, c), xi % 2)

            def latent_a(hs, coff):
                li = st['lat'] % 2
                st['lat'] += 1
                hb = B(f'hT{hs}')
                for c4 in range(4):
                    bi = 4 + st['mb'] % 3
                    st['mb'] += 1
                    mm(bank(bi), [(winA[:, kc, coff + c4 * 128:coff + (c4 + 1) * 128], hT[hs][:, kc, :])
                                  for kc in range(16)], [B('winA'), hb], PSB[bi])
                    dve(lambda e, c4=c4, bi=bi: e.tensor_copy(out=lat32[li][:, c4, :], in_=bank(bi)),
                        reads=[PSB[bi]], pwrites=[B(f'lat32{li}')])
                for c4 in range(4):
                    act(sq[li][:, c4, :], lat32[li][:, c4, :], AF.Square, reads=[B(f'lat32{li}')],
                        pwrites=[B(f'sq{li}')])
                return li

            def latent_b(t, li, gfun, dst, dstb):
                mm(bank(7), [(ones, sq[li][:, c4, :]) for c4 in range(4)], [B('ones'), B(f'sq{li}')], PSB[7])
                act(rst[li], bank(7), AF.Sqrt, reads=[PSB[7]], writes=[B(f'rst{li}')], scale=1.0 / 512, bias=EPS)
                dve(lambda e: e.reciprocal(out=rst[li], in_=rst[li]), writes=[B(f'rst{li}')])
                for c4 in range(4):
                    dve(lambda e, c4=c4: e.scalar_tensor_tensor(out=nst[li][:, c4, :], in0=lat32[li][:, c4, :],
                                                                 scalar=gfun(c4), in1=rst[li], op0=ALU.mult,
                                                                 op1=ALU.mult),
                        reads=[B(f'lat32{li}'), B(f'rst{li}'), B('vecT')], pwrites=[B(f'nst{li}')])
                dma(STQ[g], dst[:, :, t * 512:(t + 1) * 512], nst[li], reads=[B(f'nst{li}')], pwrites=[dstb],
                    owner=B(f'nst{li}'))

            def matmuls(t):
                hs = t % 3
                hb = B(f'hT{hs}')
                if t < own_tiles:
                    dma(STQ[g], h_scr[g][:, :, t * 512:(t + 1) * 512], hT[hs], reads=[hb], pwrites=[B(f'h_scr{g}')],
                        owner=hb)
                li_kv = latent_a(hs, 512)
                ts = t % 2
                dma('sp', ctab[ts][0:64, :], rope_d[g][0, :, t * 512:(t + 1) * 512], writes=[B(f'ctab{ts}')])
                dma('sp', stab[ts][0:64, :], rope_d[g][1, :, t * 512:(t + 1) * 512], writes=[B(f'stab{ts}')])
                ba = 4 + st['mb'] % 3
                st['mb'] += 1
                bb = 4 + st['mb'] % 3
                st['mb'] += 1
                mm(bank(ba)[0:64, :], [(winA[:, kc, 1024:1088], hT[hs][:, kc, :]) for kc in range(16)],
                   [B('winA'), hb], PSB[ba])
                mm(bank(bb)[0:64, :], [(winA[:, kc, 1088:1152], hT[hs][:, kc, :]) for kc in range(16)],
                   [B('winA'), hb], PSB[bb])
                dve(lambda e: e.tensor_tensor(out=r1[0:64, :], in0=bank(ba)[0:64, :], in1=ctab[ts][0:64, :],
                                              op=ALU.mult), reads=[PSB[ba], B(f'ctab{ts}')], writes=[B('r1')])
                dve(lambda e: e.tensor_tensor(out=r2[0:64, :], in0=bank(bb)[0:64, :], in1=stab[ts][0:64, :],
                                              op=ALU.mult), reads=[PSB[bb], B(f'stab{ts}')], writes=[B('r2')])
                dve(lambda e: e.tensor_tensor(out=krs[0:64, :], in0=r1[0:64, :], in1=r2[0:64, :], op=ALU.add),
                    reads=[B('r1'), B('r2')], writes=[B('krs')])
                dma(STQ[g], kr_scr[g][:, t * 512:(t + 1) * 512], krs[0:64, :], reads=[B('krs')],
                    pwrites=[B(f'kr_scr{g}')], owner=B('krs'))
                li_q = latent_a(hs, 0) if t < own_tiles else None
                latent_b(t, li_kv, GKV, kvn_scr[g], B(f'kvn_scr{g}'))
                if li_q is not None:
                    latent_b(t, li_q, GQ, qn_scr[g], B(f'qn_scr{g}'))

            transposes(0)
            if nt > 1:
                transposes(1)
            for t in range(nt):
                if t + 2 < nt:
                    transposes(t + 2)
                else:
                    NT.flush()
                matmuls(t)
            hs = nt % 3
            xi = NT.i
            xs = NT.X[xi % 3]
            xb = B(f'X{xi % 3}')
            dma('sp', xs[0:32, :], halo_d[g], writes=[xb])
            NT.norm_T(xs, xb, 32, hT[hs], B(f'hT{hs}'), 0, lambda c: G1T(g, c), lambda c: SH1T(g, c), xi % 2)
            NT.flush()
            dma(STQ[g], h_scr[g][:, :, NOWN:NOWN + 32], hT[hs][:, :, 0:32], reads=[B(f'hT{hs}')],
                pwrites=[B(f'h_scr{g}')], owner=B(f'hT{hs}'))

        HGW = NOWN + 32

        def phaseA2a(g):
            A = Arena(arena)
            hg = A.alloc(BF16, [128, 8, HGW])
            winC = A.alloc(BF16, [128, 16, 2048])
            hTl = [A.alloc(BF16, [128, 16, 512]) for _ in range(2)]
            sg = [A.alloc(F32, [128, 512]) for _ in range(2)]
            htmp = A.alloc(F32, [128, 32])
            for half in range(2):
                dma('sp', winC[:, :, half * 1024:(half + 1) * 1024],
                    wbf_in[:, half * 1024:(half + 1) * 1024].rearrange("(c p) n -> p c n", p=128),
                    reads=[B('wbf_inC')], pwrites=[B('winC')])
            k = 0
            for t in range(own_tiles + 1):
                hs = t % 2
                hb = B(f'hTl{hs}')
                halo = (t == own_tiles)
                n = 32 if halo else 512
                c0 = NOWN if halo else t * 512
                dma('sp', hTl[hs][:, :, 0:n], h_scr[g][:, :, c0:c0 + n], reads=[B(f'h_scr{g}')], writes=[hb])
                for j in range(8):
                    ba = (2 * k) % 8
                    bg = (2 * k + 1) % 8
                    si = k % 2
                    k += 1
                    mm(bank(ba)[:, 0:n], [(winC[:, kc, j * 128:(j + 1) * 128], hTl[hs][:, kc, 0:n])
                                          for kc in range(16)], [B('winC'), hb], PSB[ba])
                    mm(bank(bg)[:, 0:n], [(winC[:, kc, 1024 + j * 128:1024 + (j + 1) * 128], hTl[hs][:, kc, 0:n])
                                          for kc in range(16)], [B('winC'), hb], PSB[bg])
                    act(sg[si][:, 0:n], bank(bg)[:, 0:n], AF.Sigmoid, reads=[PSB[bg]], writes=[B(f'sg{si}')])
                    if not halo:
                        dve(lambda e, j=j, ba=ba, si=si, t=t: e.tensor_tensor(
                            out=hg[:, j, 15 + t * 512:15 + (t + 1) * 512], in0=bank(ba), in1=sg[si], op=ALU.mult),
                            reads=[PSB[ba], B(f'sg{si}')], pwrites=[B('hg')])
                    else:
                        dve(lambda e, ba=ba, si=si: e.tensor_tensor(out=htmp, in0=bank(ba)[:, 0:32],
                                                                     in1=sg[si][:, 0:32], op=ALU.mult),
                            reads=[PSB[ba], B(f'sg{si}')], writes=[B('htmp')])
                        dve(lambda e: e.tensor_tensor(out=htmp, in0=htmp, in1=hmask[:, g, :], op=ALU.mult),
                            reads=[B('hmask')], writes=[B('htmp')])
                        dve(lambda e, j=j: e.tensor_copy(out=hg[:, j, 0:15], in_=htmp[:, 0:15]),
                            reads=[B('htmp')], pwrites=[B('hg')])
                        dve(lambda e, j=j: e.tensor_copy(out=hg[:, j, 15 + NOWN:30 + NOWN], in_=htmp[:, 15:30]),
                            reads=[B('htmp')], pwrites=[B('hg')])

        def phaseA2b(g):
            A = Arena(arena)
            hg = A.alloc(BF16, [128, 8, HGW])
            dg = A.alloc(BF16, [128, 248, 128])
            y32 = A.alloc(F32, [128, 8, 512])
            ysq = A.alloc(BF16, [128, 8, 512])
            ybf = A.alloc(BF16, [128, 8, 512])
            mean = A.alloc(F32, [128, 512])
            var = A.alloc(F32, [128, 512])
            rstd = A.alloc(F32, [128, 512])
            co = [A.alloc(BF16, [128, 8, 512]) for _ in range(2)]
            for idx in range(248):
                k, c = idx // 8, idx % 8
                if idx % 2 == 0:
                    dve(lambda e, idx=idx, k=k, c=c: e.tensor_scalar(out=dg[:, idx, :], in0=identb,
                                                                      scalar1=WDW(k, c), scalar2=None, op0=ALU.mult),
                        reads=[B('identb'), B('vecT')], pwrites=[B('dg')])
                else:
                    act(dg[:, idx, :], identb, AF.Copy, reads=[B('identb'), B('vecT')], pwrites=[B('dg')],
                        scale=WDW(k, c))
            for t in range(own_tiles):
                for c in range(8):
                    bi = c % 4
                    mm(bank(bi), [(dg[:, k * 8 + c, :], hg[:, c, t * 512 + k:t * 512 + k + 512]) for k in range(31)],
                       [B('dg'), B('hg')], PSB[bi])
                    act(y32[:, c, :], bank(bi), AF.Identity, reads=[PSB[bi], B('vecT')], pwrites=[B('y32')],
                        bias=BDW(c))
                    act(ysq[:, c, :], bank(bi), AF.Square, reads=[PSB[bi], B('vecT')], pwrites=[B('ysq')],
                        bias=BDW(c))
                    dve(lambda e, c=c: e.tensor_copy(out=ybf[:, c, :], in_=y32[:, c, :]), reads=[B('y32')],
                        pwrites=[B('ybf')])
                mm(bank(4), [(ones, ybf[:, c, :]) for c in range(8)], [B('ones'), B('ybf')], PSB[4])
                mm(bank(5), [(ones, ysq[:, c, :]) for c in range(8)], [B('ones'), B('ysq')], PSB[5])
                dve(lambda e: e.tensor_scalar(out=mean, in0=bank(4), scalar1=1.0 / DC, scalar2=None, op0=ALU.mult),
                    reads=[PSB[4]], writes=[B('mean')])
                dve(lambda e: e.tensor_tensor(out=var, in0=mean, in1=mean, op=ALU.mult), reads=[B('mean')],
                    writes=[B('var')])
                dve(lambda e: e.scalar_tensor_tensor(out=var, in0=bank(5), scalar=1.0 / DC, in1=var, op0=ALU.mult,
                                                     op1=ALU.subtract), reads=[PSB[5]], writes=[B('var')])
                act(rstd, var, AF.Sqrt, reads=[B('var')], writes=[B('rstd')], bias=EPS)
                dve(lambda e: e.reciprocal(out=rstd, in_=rstd), writes=[B('rstd')])
                cs = t % 2
                for c in range(8):
                    dve(lambda e, c=c: e.tensor_tensor(out=y32[:, c, :], in0=y32[:, c, :], in1=mean,
                                                       op=ALU.subtract), reads=[B('mean')], writes=[B('y32')])
                    dve(lambda e, c=c: e.tensor_tensor(out=y32[:, c, :], in0=y32[:, c, :], in1=rstd, op=ALU.mult),
                        reads=[B('rstd')], writes=[B('y32')])
                    act(co[cs][:, c, :], y32[:, c, :], AF.Silu, reads=[B('y32'), B('vecT')], pwrites=[B(f'co{cs}')],
                        scale=GCV(c), bias=BCV(c))
                dma(STQ[g], conv_scr[g][:, :, t * 512:(t + 1) * 512], co[cs], reads=[B(f'co{cs}')],
                    pwrites=[B(f'conv_scr{g}')], owner=B(f'co{cs}'))

        def phaseB(g):
            A = Arena(arena)
            nct = ctx_tiles[g]
            nkc = nct * 4
            wq = A.alloc(BF16, [128, 4, 2048])
            wkv = A.alloc(BF16, [128, 4, 2048])
            qn = A.alloc(BF16, [128, 4, NOWN])
            kr = A.alloc(BF16, [128, NCTX[g]])
            cq = A.alloc(F32, [128, NOWN])
            sqt = A.alloc(F32, [128, NOWN])
            Kh = A.alloc(BF16, [128, NCTX[g]])
            Vh = A.alloc(BF16, [128, nkc, 128])
            kvt = [A.alloc(BF16, [128, 4, 512]) for _ in range(3)]
            qnope = [A.alloc(BF16, [128, 512]) for _ in range(2)]
            qrope = [A.alloc(BF16, [128, 512]) for _ in range(2)]
            pT = [A.alloc(BF16, [128, 512]) for _ in range(4)]
            ptmp = A.alloc(BF16, [128, 512])
            r1 = A.alloc(F32, [128, 512])
            r2 = A.alloc(F32, [128, 512])
            rs = A.alloc(F32, [128, 512])
            ao = [A.alloc(BF16, [128, 512]) for _ in range(2)]
            acc = [A.alloc(F32, [128, 512]) for _ in range(2)]
            shi = A.alloc(BF16, [128, 512])
            slo = A.alloc(BF16, [128, 512])
            dma('sp', wq, wbf_uq.rearrange("(c p) n -> p c n", p=128), reads=[B('wbf_uq')], writes=[B('wq')])
            dma('sp', wkv, wbf_ukv.rearrange("(c p) n -> p c n", p=128), reads=[B('wbf_ukv')], writes=[B('wkv')])
            dma('sp', qn, qn_scr[g], reads=[B(f'qn_scr{g}')], writes=[B('qn')])
            dma('sp', kr[0:64, :], kr_scr[g], reads=[B(f'kr_scr{g}')], writes=[B('kr')])
            dma('sp', cq[0:64, :], rope_d[g][0, :, 0:NOWN], writes=[B('cq')])
            dma('sp', sqt[0:64, :], rope_d[g][1, :, 0:NOWN], writes=[B('sqt')])
            kvi = 0
            pti = 0
            qi = 0
            sti = 0
            for h in range(NH):
                if g == 0:
                    emit_bg((len(bg_chunks) + (NH - h) - 1) // (NH - h), [B('Kh')])
                for t in range(nct):
                    ks = kvi % 3
                    kvi += 1
                    kb = B(f'kvt{ks}')
                    dma('sp', kvt[ks], kvn_scr[g][:, :, t * 512:(t + 1) * 512], reads=[B(f'kvn_scr{g}')],
                        writes=[kb])
                    mm(bank(6), [(wkv[:, c4, h * 256:h * 256 + 128], kvt[ks][:, c4, :]) for c4 in range(4)],
                       [B('wkv'), kb], PSB[6])
                    act(Kh[:, t * 512:(t + 1) * 512], bank(6), AF.Copy, reads=[PSB[6]], pwrites=[B('Kh')])
                    vb = bank(7).rearrange("p (s d) -> p s d", s=4)

                    def vfn(e, ks=ks, vb=vb, h=h):
                        inst = None
                        for s in range(4):
                            for c4 in range(4):
                                inst = e.matmul(vb[:, s, :], lhsT=kvt[ks][:, c4, s * 128:(s + 1) * 128],
                                                rhs=wkv[:, c4, h * 256 + 128:h * 256 + 256],
                                                start=(c4 == 0), stop=(c4 == 3))
                        return inst
                    P.op('pe', vfn, reads=[B('wkv'), kb], writes=[PSB[7]])
                    dve(lambda e, t=t, vb=vb: e.tensor_copy(out=Vh[:, t * 4:(t + 1) * 4, :], in_=vb),
                        reads=[PSB[7]], pwrites=[B('Vh')])
                for j in range(own_tiles):
                    qs = qi % 2
                    qi += 1
                    jsl = slice(j * 512, (j + 1) * 512)
                    mm(bank(6), [(wq[:, c4, h * 192:h * 192 + 128], qn[:, c4, jsl]) for c4 in range(4)],
                       [B('wq'), B('qn')], PSB[6])
                    act(qnope[qs], bank(6), AF.Copy, reads=[PSB[6]], writes=[B(f'qnope{qs}')])
                    mm(bank(7)[0:64, :], [(wq[:, c4, h * 192 + 128:h * 192 + 192], qn[:, c4, jsl]) for c4 in range(4)],
                       [B('wq'), B('qn')], PSB[7])
                    mm(bank(6)[0:64, :], [(wq[:, c4, 1536 + h * 64:1536 + (h + 1) * 64], qn[:, c4, jsl])
                                          for c4 in range(4)], [B('wq'), B('qn')], PSB[6])
                    dve(lambda e, jsl=jsl: e.tensor_tensor(out=r1[0:64, :], in0=bank(7)[0:64, :], in1=cq[0:64, jsl],
                                                           op=ALU.mult), reads=[PSB[7], B('cq')], writes=[B('r1')])
                    dve(lambda e, jsl=jsl: e.tensor_tensor(out=r2[0:64, :], in0=bank(6)[0:64, :], in1=sqt[0:64, jsl],
                                                           op=ALU.mult), reads=[PSB[6], B('sqt')], writes=[B('r2')])
                    dve(lambda e, qs=qs: e.tensor_tensor(out=qrope[qs][0:64, :], in0=r1[0:64, :], in1=r2[0:64, :],
                                                         op=ALU.add), reads=[B('r1'), B('r2')],
                        writes=[B(f'qrope{qs}')])
                    ob = 3 + (qi % 2)
                    sb_ = 5
                    ac = acc[qi % 2]
                    acb = B(f'acc{qi % 2}')
                    qdeps = [B('Kh'), B('kr'), B(f'qnope{qs}'), B(f'qrope{qs}')]

                    def st_op(kc):
                        nonlocal sti
                        stb = sti % 3
                        sti += 1
                        ksl = slice(kc * 128, (kc + 1) * 128)
                        mm(bank(stb), [(Kh[:, ksl], qnope[qs]), (kr[0:64, ksl], qrope[qs][0:64, :])], qdeps, PSB[stb])
                        return stb
                    pprev = 0
                    stq = [st_op(0)]
                    if nkc > 1:
                        stq.append(st_op(1))
                    for kc in range(nkc):
                        stb = stq.pop(0)
                        if kc + 2 < nkc:
                            stq.append(st_op(kc + 2))
                        ps_ = pti % 4
                        pti += 1
                        act(pT[ps_], bank(stb), AF.Exp, reads=[PSB[stb]], writes=[B(f'pT{ps_}')], scale=SCALE)

                        def pv(e, kc=kc, ps_=ps_, ob=ob, sb_=sb_):
                            e.matmul(bank(ob), lhsT=Vh[:, kc, :], rhs=pT[ps_], start=(kc == 0), stop=(kc == nkc - 1))
                            return e.matmul(bank(sb_), lhsT=ones, rhs=pT[ps_], start=(kc == 0),
                                            stop=(kc == nkc - 1))
                        if kc == 0:
                            P.op('pe', pv, reads=[B('Vh'), B('ones'), B(f'pT{ps_}')], writes=[PSB[ob], PSB[sb_]])
                        else:
                            P.op('pe', pv, reads=[B('Vh'), B('ones'), B(f'pT{ps_}')], pwrites=[PSB[ob], PSB[sb_]])
                    dve(lambda e, sb_=sb_: e.reciprocal(out=rs, in_=bank(sb_)), reads=[PSB[sb_]], writes=[B('rs')])
                    a_ = qi % 2
                    dve(lambda e, ob=ob, a_=a_: e.tensor_tensor(out=ao[a_], in0=bank(ob), in1=rs, op=ALU.mult),
                        reads=[PSB[ob], B('rs')], writes=[B(f'ao{a_}')])
                    dma(STQ[g], attn_scr[g][:, h, jsl], ao[a_], reads=[B(f'ao{a_}')], pwrites=[B(f'attn_scr{g}')],
                        owner=B(f'ao{a_}'))

        def phaseC1(g):
            A = Arena(arena)
            wo = A.alloc(BF16, [128, 16, 2048])
            NT = NormT(A, nx=4)
            mixT = [A.alloc(BF16, [128, 16, 512]) for _ in range(2)]
            mo = [A.alloc(F32, [128, D]) for _ in range(3)]
            gt1 = A.alloc(F32, [128, D])
            h2T = [A.alloc(BF16, [128, 16, 512]) for _ in range(1)]
            for half in range(2):
                dma('pool', wo[:, :, half * 1024:(half + 1) * 1024],
                    wbf_out[:, half * 1024:(half + 1) * 1024].rearrange("(c p) n -> p c n", p=128),
                    reads=[B('wbf_out')], pwrites=[B('wo')])
            dma('sp', gt1, modscr[g, 2, :].partition_broadcast(128), reads=[B('modscr')], writes=[B('gt1')])
            nsub = own_tiles * 4

            def mo_stage(i):
                t, s_ = i // 4, i % 4
                ms = t % 2
                mb = B(f'mixT{ms}')
                if s_ == 0:
                    tsl = slice(t * 512, (t + 1) * 512)
                    dma('sp', mixT[ms][:, 0:8, :], conv_scr[g][:, :, tsl], reads=[B(f'conv_scr{g}')], pwrites=[mb])
                    dma('sp', mixT[ms][:, 8:16, :], attn_scr[g][:, :, tsl], reads=[B(f'attn_scr{g}')], pwrites=[mb])
                xs = NT.X[i % 4]
                xb = B(f'X{i % 4}')
                r0 = i * 128
                dma('sp', xs, x_d[g][r0:r0 + 128, :], writes=[xb])
                m = mo[i % 3]
                mob = B(f'mo{i % 3}')
                for cb in range(4):
                    bi = 4 + cb
                    mm(bank(bi), [(mixT[ms][:, kc, s_ * 128:(s_ + 1) * 128], wo[:, kc, cb * 512:(cb + 1) * 512])
                                  for kc in range(16)], [mb, B('wo')], PSB[bi])
                    if cb % 2 == 0:
                        act(m[:, cb * 512:(cb + 1) * 512], bank(bi), AF.Copy, reads=[PSB[bi]], pwrites=[mob])
                    else:
                        dve(lambda e, m=m, cb=cb, bi=bi: e.tensor_copy(out=m[:, cb * 512:(cb + 1) * 512],
                                                                       in_=bank(bi)),
                            reads=[PSB[bi]], pwrites=[mob])

            def post_stage(i):
                t, s_ = i // 4, i % 4
                hs = 0
                xs = NT.X[i % 4]
                xb = B(f'X{i % 4}')
                r0 = i * 128
                m = mo[i % 3]
                mob = B(f'mo{i % 3}')
                sc, sb = NT.stats_rstd(m, mob, 128, 4 + i % 2)
                dve(lambda e, m=m, sc=sc: e.scalar_tensor_tensor(out=m, in0=m, scalar=sc, in1=gt1, op0=ALU.mult,
                                                                  op1=ALU.mult),
                    reads=[sb, B('gt1')], writes=[mob])
                dve(lambda e, m=m, xs=xs: e.tensor_tensor(out=xs, in0=xs, in1=m, op=ALU.add), reads=[mob],
                    writes=[xb])
                dma(STQ[g], x1_scr[g][r0:r0 + 128, :], xs, reads=[xb], pwrites=[B(f'x1_scr{g}')], owner=xb)
                NT.norm_T(xs, xb, 128, h2T[hs], B(f'h2T{hs}'), s_ * 128, lambda c: G2T(g, c),
                          lambda c: SH2T(g, c), i % 2)
                if i >= 1 and (i - 1) % 4 == 3:
                    store_h2((i - 1) // 4)

            def store_h2(t):
                tsl = slice(t * 512, (t + 1) * 512)
                dma(STQ[g], h2_scr[g][:, :, tsl], h2T[0], reads=[B('h2T0')], pwrites=[B(f'h2_scr{g}')], owner=B('h2T0'))

            mo_stage(0)
            if nsub > 1:
                mo_stage(1)
            for i in range(nsub):
                if i + 2 < nsub:
                    mo_stage(i + 2)
                post_stage(i)
            NT.flush()
            store_h2(own_tiles - 1)

        def phaseC2(g):
            A = Arena(arena)
            h2T = [A.alloc(BF16, [128, 16, 512]) for _ in range(2)]
            actT = A.alloc(BF16, [128, NJ, 512])
            wgu = [A.alloc(BF16, [128, 16, 2, 256]) for _ in range(2)]
            wdn = [A.alloc(BF16, [128, 11, 512]) for _ in range(2)]
            f32 = [A.alloc(F32, [128, D]) for _ in range(4)]
            X1 = [A.alloc(F32, [128, D]) for _ in range(2)]
            gt2 = A.alloc(F32, [128, D])
            junk = A.alloc(BF16, [128, D])
            sg = [A.alloc(F32, [128, 512]) for _ in range(2)]
            dma('sp', gt2, modscr[g, 5, :].partition_broadcast(128), reads=[B('modscr')], writes=[B('gt2')])
            gi = 0
            di = 0
            ki = 0
            xi = 0
            pending = []

            def epilogue(t, s):
                nonlocal xi
                if True:
                    r0 = t * 512 + s * 128
                    x1 = X1[xi % 2]
                    x1b = B(f'X1{xi % 2}')
                    xi += 1
                    dma('sp', x1, x1_scr[g][r0:r0 + 128, :], reads=[B(f'x1_scr{g}')], writes=[x1b])
                    fb = B(f'f32{s}')
                    f = f32[s]
                    sc = ssx[:, 6 + s % 2:7 + s % 2]
                    sb = B(f'ssx{6 + s % 2}')
                    act(junk, f, AF.Square, reads=[fb], writes=[sb], accum_out=sc)
                    act(sc, sc, AF.Sqrt, writes=[sb], scale=1.0 / D, bias=EPS)
                    dve(lambda e, sc=sc: e.reciprocal(out=sc, in_=sc), writes=[sb])
                    dve(lambda e, f=f, sc=sc: e.scalar_tensor_tensor(out=f, in0=f, scalar=sc, in1=gt2, op0=ALU.mult,
                                                                      op1=ALU.mult),
                        reads=[sb, B('gt2')], writes=[fb])
                    dve(lambda e, f=f, x1=x1: e.tensor_tensor(out=f, in0=f, in1=x1, op=ALU.add), reads=[x1b],
                        writes=[fb])
                    dma('sp', y_d[g][r0:r0 + 128, :], f, reads=[fb], pwrites=[B(f'y{g}')], owner=fb)

            for t in range(own_tiles):
                hb = B(f'h2Tl{t % 2}')
                h2 = h2T[t % 2]
                if t == 0:
                    dma('sp', h2, h2_scr[g][:, :, 0:512], reads=[B(f'h2_scr{g}')], writes=[hb])
                for jb in range(NJ // 2):
                    if pending and jb in (3, 7, 11, 15):
                        epilogue(*pending.pop(0))
                    ws = gi % 2
                    gi += 1
                    wb = B(f'wgu{ws}')
                    for u in range(2):
                        c0 = u * DFF + jb * 256
                        dma('pool', wgu[ws][:, :, u, :], wbf_gu[:, c0:c0 + 256].rearrange("(c p) n -> p c n", p=128),
                            reads=[B('wbf_gu')], pwrites=[wb])
                    for jj in range(2):
                        j = jb * 2 + jj
                        bg = (2 * ki) % 4
                        bu = bg + 1
                        si = ki % 2
                        ki += 1
                        mm(bank(bg), [(wgu[ws][:, kc, 0, jj * 128:(jj + 1) * 128], h2[:, kc, :])
                                      for kc in range(16)], [wb, hb], PSB[bg])
                        mm(bank(bu), [(wgu[ws][:, kc, 1, jj * 128:(jj + 1) * 128], h2[:, kc, :])
                                      for kc in range(16)], [wb, hb], PSB[bu])
                        act(sg[si], bank(bg), AF.Silu, reads=[PSB[bg]], writes=[B(f'sg{si}')])
                        dve(lambda e, j=j, bu=bu, si=si: e.tensor_tensor(out=actT[:, j, :], in0=bank(bu), in1=sg[si],
                                                                         op=ALU.mult),
                            reads=[PSB[bu], B(f'sg{si}')], pwrites=[B('actT')])
                if t + 1 < own_tiles:
                    dma('sp', h2T[(t + 1) % 2], h2_scr[g][:, :, (t + 1) * 512:(t + 2) * 512],
                        reads=[B(f'h2_scr{g}')], writes=[B(f'h2Tl{(t + 1) % 2}')])
                for cb in range(4):
                    for kg in range(4):
                        ws = di % 2
                        di += 1
                        wb = B(f'wdn{ws}')
                        dma('pool', wdn[ws], wbf_dn[kg * 1408:(kg + 1) * 1408, cb * 512:(cb + 1) * 512].rearrange(
                            "(c p) n -> p c n", p=128), reads=[B('wbf_dn')], writes=[wb])

                        def dfn(e, ws=ws, kg=kg):
                            inst = None
                            for s in range(4):
                                for kc in range(11):
                                    inst = e.matmul(bank(4 + s), lhsT=actT[:, kg * 11 + kc, s * 128:(s + 1) * 128],
                                                    rhs=wdn[ws][:, kc, :], start=(kg == 0 and kc == 0),
                                                    stop=(kg == 3 and kc == 10))
                            return inst
                        if kg == 0:
                            P.op('pe', dfn, reads=[B('actT'), wb], writes=[PSB[4 + s] for s in range(4)])
                        else:
                            P.op('pe', dfn, reads=[B('actT'), wb], pwrites=[PSB[4 + s] for s in range(4)])
                    for s in range(4):
                        o = f32[s][:, cb * 512:(cb + 1) * 512]
                        if s % 2 == 0:
                            act(o, bank(4 + s), AF.Copy, reads=[PSB[4 + s]], pwrites=[B(f'f32{s}')])
                        else:
                            dve(lambda e, o=o, s=s: e.tensor_copy(out=o, in_=bank(4 + s)), reads=[PSB[4 + s]],
                                pwrites=[B(f'f32{s}')])
                for s in range(4):
                    pending.append((t, s))
            while pending:
                epilogue(*pending.pop(0))

        vr = setup()
        if 'cs' in phases:
            convert_pre()
        if '0' in phases:
            phase0(vr)
        P.barrier()
        if 'cb' in phases:
            convert_bg_early()
        for g in range(2):
            if 'A1' in phases:
                phaseA1(g)
                P.barrier()
            if 'A2' in phases:
                phaseA2a(g)
                P.barrier()
                phaseA2b(g)
                P.barrier()
            if 'B' in phases:
                phaseB(g)
                P.barrier()
            if 'C1' in phases:
                emit_bg(len(bg_chunks), [])
                phaseC1(g)
                P.barrier()
            if 'C2' in phases:
                phaseC2(g)
                P.barrier()

        names = P.sem_names()
        assert len(names) <= 140, len(names)
        sems = {n: es.enter_context(nc.semaphore(n)) for n in names}
        with nc.Block() as block:
            @block.tensor
            def _(e):
                P.replay('pe', e, sems)

            @block.scalar
            def _(e):
                P.replay('act', e, sems)

            @block.vector
            def _(e):
                P.replay('dve', e, sems)

            @block.gpsimd
            def _(e):
                P.replay('pool', e, sems)

            @block.sync
            def _(e):
                P.replay('sp', e, sems)
    nc._prog_stats = {e: len(P.ops[e]) for e in ENGS}
    nc._nsems = len(names)
    return nc


def _rope_tables(pos):
    inv = (1.0 / (10000.0 ** (np.arange(0, 64, 2, dtype=np.float32) / np.float32(64)))).astype(np.float32)
    ang = pos.astype(np.float32)[:, None] * inv[None, :]
    cos = np.cos(ang).astype(np.float32).T
    sin = np.sin(ang).astype(np.float32).T
    return np.ascontiguousarray(np.stack([np.concatenate([cos, cos], 0), np.concatenate([-sin, sin], 0)], 0))


def make_in_maps(inputs, own_tiles=4, ctx_tiles=(4, 16)):
    NOWN = own_tiles * 512
    xp = np.asarray(inputs['x_prompt'], np.float32)
    xs = np.asarray(inputs['x_sample'], np.float32)
    cp = np.asarray(inputs['c_prompt'], np.float32)
    cs = np.asarray(inputs['c_sample'], np.float32)
    wnames = ['w_ada', 'b_ada', 'g_pre_mix', 'g_post_mix', 'w_in', 'w_dw', 'b_dw', 'g_conv', 'b_conv', 'g_q_lat',
              'w_uq', 'g_kv_lat', 'w_ukv', 'w_out', 'g_pre_ffn', 'g_post_ffn', 'w_gate_up', 'w_down']
    shared = {n: np.ascontiguousarray(np.asarray(inputs[n], np.float32)[0]) for n in wnames}
    shared['ident'] = np.eye(128, dtype=np.float32)
    nblk = 8192 // 2048
    in_maps = []
    for c in range(8):
        sqi, qd = c // 4, c % 4
        order = [qd] + [b for b in range(nblk) if b != qd]
        pos = np.concatenate([np.arange(b * 2048, (b + 1) * 2048) for b in order])
        xS = np.concatenate([xs[sqi, b * 2048:(b + 1) * 2048] for b in order], 0)
        nS = ctx_tiles[1] * 512
        nP = ctx_tiles[0] * 512
        if own_tiles < 4:
            o0 = qd * 2048
            own = np.arange(o0, o0 + NOWN)
            rest = np.array([p for p in range(8192) if not (o0 <= p < o0 + NOWN)])
            pos = np.concatenate([own, rest])
            xS = xs[sqi][pos]
        halo = np.zeros((2, 32, D), np.float32)
        hmask = np.zeros((128, 2, 32), np.float32)
        o0 = qd * 2048
        if o0 > 0:
            halo[1, 0:15] = xs[sqi, o0 - 15:o0]
            hmask[:, 1, 0:15] = 1.0
        if o0 + NOWN < 8192:
            halo[1, 15:30] = xs[sqi, o0 + NOWN:o0 + NOWN + 15]
            hmask[:, 1, 15:30] = 1.0
        if NOWN < 2048:
            halo[0, 15:30] = xp[c, NOWN:NOWN + 15]
            hmask[:, 0, 15:30] = 1.0
        m = dict(shared)
        m['xP'] = np.ascontiguousarray(xp[c, :nP])
        m['xS'] = np.ascontiguousarray(xS[:nS])
        m['halo'] = halo
        m['hmask'] = hmask
        m['cpair'] = np.ascontiguousarray(np.stack([cp[c], cs[sqi]], 0))
        m['ropeP'] = np.ascontiguousarray(_rope_tables(np.arange(nP)))
        m['ropeS'] = np.ascontiguousarray(_rope_tables(pos[:nS]))
        in_maps.append(m)
    return in_maps


_NC_CACHE = {}


def kernel(**inputs):
    if 'full' not in _NC_CACHE:
        _NC_CACHE['full'] = build_nc()
    nc = _NC_CACHE['full']
    in_maps = make_in_maps(inputs)
    res = run_bass_kernel_spmd(nc, in_maps, core_ids=list(range(8)))
    y_prompt = np.zeros((8, 2048, D), np.float32)
    y_sample = np.zeros((2, 8192, D), np.float32)
    for c in range(8):
        r = res.results[c]
        y_prompt[c] = r['yP']
        y_sample[c // 4, (c % 4) * 2048:(c % 4 + 1) * 2048] = r['yS']
    return (y_prompt, y_sample)
```

```python
import numpy as np
from contextlib import ExitStack
import concourse.bass as bass
import concourse.mybir as mybir
from concourse.bass_utils import run_bass_kernel_spmd

F32 = mybir.dt.float32
BF16 = mybir.dt.bfloat16
AF = mybir.ActivationFunctionType
ALU = mybir.AluOpType

D = 2048
DC = 1024
NH = 8
DIN = 3136
DFF = 5632
NJ = DFF // 128
EPS = 1e-6
SCALE = 192 ** -0.5
ARENA_BYTES = 207 * 1024
G_END = 8 * 1024
ENGS = ('pe', 'act', 'dve', 'pool', 'sp')


class Buf:
    __slots__ = ('name', 'w', 'r', 'dcount', 'excl', 'bg')

    def __init__(self, name):
        self.name = name
        self.w = {}
        self.r = {}
        self.dcount = 0
        self.bg = False
        self.excl = False


def _merge(d, ev):
    s, v = ev
    if d.get(s, 0) < v:
        d[s] = v


class Prog:
    def __init__(self):
        self.ops = {e: [] for e in ENGS}
        self.cnt = {e: 0 for e in ENGS}
        self.seen = {e: {} for e in ENGS}
        self.bufs = {}
        self.dsems = {}

    def B(self, name):
        b = self.bufs.get(name)
        if b is None:
            b = Buf(name)
            self.bufs[name] = b
        return b

    def _waits(self, eng, evs):
        out = []
        seen = self.seen[eng]
        for s, v in evs.items():
            if seen.get(s, 0) >= v:
                continue
            seen[s] = v
            out.append((s, v))
        return out

    def _deps(self, reads, writes, pwrites):
        evs = {}
        for b in reads:
            for ev in b.w.items():
                _merge(evs, ev)
        for b in writes:
            for ev in b.w.items():
                _merge(evs, ev)
            for ev in b.r.items():
                _merge(evs, ev)
        for b in pwrites:
            for ev in b.r.items():
                _merge(evs, ev)
        return evs

    def _update(self, ev, reads, writes, pwrites):
        for b in reads:
            _merge(b.r, ev)
        for b in writes:
            b.w = {ev[0]: ev[1]}
            b.r = {}
        for b in pwrites:
            _merge(b.w, ev)

    def op(self, eng, fn, reads=(), writes=(), pwrites=()):
        if any(b.excl for b in reads):
            writes = list(writes) + [b for b in reads if b.excl]
            reads = [b for b in reads if not b.excl]
        evs = self._deps(reads, writes, pwrites)
        self.cnt[eng] += 1
        ev = ('E' + eng, self.cnt[eng])
        self.ops[eng].append((self._waits(eng, evs), fn, ev[0], 1))
        self._update(ev, reads, writes, pwrites)
        return ev

    def dma(self, q, fn, reads=(), writes=(), pwrites=(), owner=None):
        dst = owner if owner is not None else (writes[0] if writes else pwrites[0])
        evs = self._deps(reads, writes, pwrites)
        cls = 's' if q == 'pool' else 'h'
        if dst.dcount == 0:
            dst.dcount = {}
        dst.dcount[cls] = dst.dcount.get(cls, 0) + 16
        sem = 'D' + dst.name + '_' + cls
        self.dsems[sem] = (dst, cls)
        ev = (sem, dst.dcount[cls])
        self.ops[q].append((self._waits(q, evs), fn, sem, 16))
        self._update(ev, reads, writes, pwrites)
        return ev

    def barrier(self):
        evs = {}
        for e in ENGS:
            if self.cnt[e] > 0:
                evs['E' + e] = self.cnt[e]
        for sem, (b, cls) in self.dsems.items():
            if not b.bg:
                evs[sem] = b.dcount[cls]
        for e in ENGS:
            self.ops[e].append((self._waits(e, evs), None, None, 0))
        for b in self.bufs.values():
            if not b.bg:
                b.w = {}
                b.r = {}

    def sem_names(self):
        names = ['E' + e for e in ENGS if self.cnt[e] > 0]
        names += list(self.dsems.keys())
        return names

    def replay(self, eng, e, sems):
        for waits, fn, sem, inc in self.ops[eng]:
            for s, v in waits:
                e.wait_ge(sems[s], v)
            if fn is not None:
                inst = fn(e)
                inst.then_inc(sems[sem], inc)


class Arena:
    def __init__(self, t, base=G_END):
        self.t = t
        self.off = base

    def alloc(self, dtype, shape):
        n = 1
        for s in shape[1:]:
            n *= s
        nbytes = n * (2 if dtype == BF16 else 4)
        nbytes = (nbytes + 63) // 64 * 64
        off = self.off
        self.off += nbytes
        assert self.off <= ARENA_BYTES, f"arena overflow {self.off}"
        v = self.t[:, off // 4:(off + nbytes) // 4]
        if dtype == BF16:
            v = v.bitcast(BF16)
        v = v[:, 0:n]
        if len(shape) == 3:
            v = v.rearrange("p (a b) -> p a b", a=shape[1])
        elif len(shape) == 4:
            v = v.rearrange("p (a b c) -> p a b c", a=shape[1], b=shape[2])
        return v


def build_nc(own_tiles=4, ctx_tiles=(4, 16), debug=False, phases='cs0cbA1A2BC1C2'):
    NOWN = own_tiles * 512
    NCTX = [ctx_tiles[0] * 512, ctx_tiles[1] * 512]
    nc = bass.Bass("TRN2", target_bir_lowering=False)

    def din(name, shape, dt=F32):
        return nc.dram_tensor(name, shape, dt, kind="ExternalInput").ap()

    def dscr(name, shape, dt):
        return nc.dram_tensor(name, shape, dt, kind=("ExternalOutput" if debug else "Internal")).ap()

    x_d = [din("xP", [NCTX[0], D]), din("xS", [NCTX[1], D])]
    halo_d = din("halo", [2, 32, D])
    hmask_d = din("hmask", [128, 2, 32])
    cpair_d = din("cpair", [2, D])
    rope_d = [din("ropeP", [2, 64, NCTX[0]]), din("ropeS", [2, 64, NCTX[1]])]
    ident_d = din("ident", [128, 128])
    w_ada = din("w_ada", [D, 6 * D])
    b_ada = din("b_ada", [6 * D])
    g_pre_mix = din("g_pre_mix", [D])
    g_post_mix = din("g_post_mix", [D])
    w_in = din("w_in", [D, DIN])
    w_dw = din("w_dw", [31, DC])
    b_dw = din("b_dw", [DC])
    g_conv = din("g_conv", [DC])
    b_conv = din("b_conv", [DC])
    g_q_lat = din("g_q_lat", [512])
    w_uq = din("w_uq", [512, 1536])
    g_kv_lat = din("g_kv_lat", [512])
    w_ukv = din("w_ukv", [512, 2048])
    w_out = din("w_out", [D, D])
    g_pre_ffn = din("g_pre_ffn", [D])
    g_post_ffn = din("g_post_ffn", [D])
    w_gu = din("w_gate_up", [D, 2 * DFF])
    w_dn = din("w_down", [DFF, D])

    y_d = [nc.dram_tensor("yP", [NOWN, D], F32, kind="ExternalOutput").ap(),
           nc.dram_tensor("yS", [NOWN, D], F32, kind="ExternalOutput").ap()]

    wbf_in = dscr("wbf_in", [D, 3200], BF16)
    wbf_uq = dscr("wbf_uq", [512, 2048], BF16)
    wbf_ukv = dscr("wbf_ukv", [512, 2048], BF16)
    wbf_out = dscr("wbf_out", [D, D], BF16)
    wbf_gu = dscr("wbf_gu", [D, 2 * DFF], BF16)
    wbf_dn = dscr("wbf_dn", [DFF, D], BF16)
    modscr = dscr("modscr", [2, 6, D], F32)
    h_scr = [dscr(f"h_scr{g}", [128, 16, NOWN + 32], BF16) for g in range(2)]
    kvn_scr = [dscr(f"kvn_scr{g}", [128, 4, NCTX[g]], BF16) for g in range(2)]
    kr_scr = [dscr(f"kr_scr{g}", [64, NCTX[g]], BF16) for g in range(2)]
    qn_scr = [dscr(f"qn_scr{g}", [128, 4, NOWN], BF16) for g in range(2)]
    conv_scr = [dscr(f"conv_scr{g}", [128, 8, NOWN], BF16) for g in range(2)]
    attn_scr = [dscr(f"attn_scr{g}", [128, 8, NOWN], BF16) for g in range(2)]
    x1_scr = [dscr(f"x1_scr{g}", [NOWN, D], F32) for g in range(2)]
    h2_scr = [dscr(f"h2_scr{g}", [128, 16, NOWN], BF16) for g in range(2)]

    P = Prog()
    B = P.B

    with ExitStack() as es:
        arena = es.enter_context(nc.sbuf_tensor("arena", [128, ARENA_BYTES // 4], F32))
        psum = es.enter_context(nc.psum_tensor("psum", [128, 8, 512], F32))

        def bank(i):
            return psum[:, i, :]

        def ptview(i):
            return psum[:, 2 * i:2 * i + 2, :].rearrange("p a b -> p (a b)").bitcast(BF16).rearrange(
                "p (c n) -> p c n", c=16)

        PSB = [B(f"ps{i}") for i in range(8)]
        STQ = ['sp', 'pool']
        for b_ in PSB:
            b_.excl = True

        GA = Arena(arena, 0)
        identf = GA.alloc(F32, [128, 128])
        identb = GA.alloc(BF16, [128, 128])
        ones = GA.alloc(BF16, [128, 128])
        onesf = GA.alloc(F32, [128, 128])
        vecT = GA.alloc(F32, [128, 6, 128])
        hmask = GA.alloc(F32, [128, 2, 32])
        ssx = GA.alloc(F32, [128, 8])
        assert GA.off <= G_END

        def G1T(g, c): return vecT[:, g, 16 + c:17 + c]
        def SH1T(g, c): return vecT[:, g, c:c + 1]
        def G2T(g, c): return vecT[:, g, 64 + c:65 + c]
        def SH2T(g, c): return vecT[:, g, 48 + c:49 + c]
        def GQ(c): return vecT[:, 2, c:c + 1]
        def GKV(c): return vecT[:, 2, 4 + c:5 + c]
        def BDW(c): return vecT[:, 2, 8 + c:9 + c]
        def GCV(c): return vecT[:, 2, 16 + c:17 + c]
        def BCV(c): return vecT[:, 2, 24 + c:25 + c]
        def WDW(k, c):
            idx = k * 8 + c
            return vecT[:, 3 + idx // 128, idx % 128:idx % 128 + 1]

        def dma(q, out, in_, reads=(), writes=(), pwrites=(), owner=None, **kw):
            return P.dma(q, lambda e: e.dma_start(out=out, in_=in_, **kw), reads=reads, writes=writes,
                         pwrites=pwrites, owner=owner)

        def act(out, in_, func, reads=(), writes=(), pwrites=(), **kw):
            return P.op('act', lambda e: e.activation(out=out, in_=in_, func=func, **kw), reads=reads,
                        writes=writes, pwrites=pwrites)

        def dve(fn, reads=(), writes=(), pwrites=()):
            return P.op('dve', fn, reads=reads, writes=writes, pwrites=pwrites)

        def mm(out, pairs, reads, wbuf, start=True, stop=True):
            def fn(e):
                n = len(pairs)
                inst = None
                for i, (l, r) in enumerate(pairs):
                    inst = e.matmul(out, lhsT=l, rhs=r, start=(start and i == 0), stop=(stop and i == n - 1))
                return inst
            return P.op('pe', fn, reads=reads, writes=[wbuf])

        def setup():
            A = Arena(arena)
            vr = A.alloc(F32, [128, 6, 128])
            dma('sp', identf, ident_d[:, :], writes=[B('identf')])
            dma('sp', hmask, hmask_d[:, :, :], writes=[B('hmask')])
            dve(lambda e: e.tensor_copy(out=identb, in_=identf), reads=[B('identf')], writes=[B('identb')])
            dve(lambda e: e.memset(ones, 1.0), writes=[B('ones')])
            dve(lambda e: e.memset(onesf, 1.0), writes=[B('onesf')])
            rows = [(g_q_lat, 4), (g_kv_lat, 4), (b_dw, 8), (g_conv, 8), (b_conv, 8)]
            r0 = 0
            for src, n in rows:
                dma('sp', vr[r0:r0 + n, 2, :], src.rearrange("(c p) -> c p", p=128), pwrites=[B('vr2')])
                r0 += n
            wv = w_dw.rearrange("k (c p) -> (k c) p", p=128)
            dma('sp', vr[:, 3, :], wv[0:128, :], writes=[B('vr3')])
            dma('sp', vr[0:120, 4, :], wv[128:248, :], writes=[B('vr4')])
            dma('sp', vr[0:32, 5, :], cpair_d.rearrange("g (c p) -> (g c) p", p=128), writes=[B('vr5')])
            for i, n in ((2, 32), (3, 128), (4, 120), (5, 32)):
                mm(bank(i)[:, 0:n], [(vr[0:n, i, :], identf[0:n, 0:n])], [B(f'vr{i}'), B('identf')], PSB[i])
                dve(lambda e, i=i, n=n: e.tensor_copy(out=vecT[:, i, 0:n], in_=bank(i)[:, 0:n]),
                    reads=[PSB[i]], pwrites=[B('vecT')])
            return vr

        def convert(dst, src, nrows, ncols, name, rows_per=128):
            for r in range(0, nrows, rows_per):
                dma('pool', dst[r:r + rows_per, 0:ncols], src[r:r + rows_per, 0:ncols], pwrites=[B(name)])

        def convert_pre():
            for r in range(0, D, 512):
                dma('pool', wbf_in[r:r + 512, 2048:3136], w_in[r:r + 512, 2048:3136], pwrites=[B('wbf_inA')])
            dma('pool', wbf_in[:, 3136:3168], w_in[:, 3104:3136], pwrites=[B('wbf_inA')])
            dma('pool', wbf_in[:, 3168:3200], w_in[:, 3072:3104], pwrites=[B('wbf_inA')])

        bg_chunks = []

        def convert_bg_early():
            for n_ in ('wbf_inC', 'wbf_uq', 'wbf_ukv', 'wbf_out', 'wbf_gu', 'wbf_dn'):
                B(n_).bg = True
            for r in range(0, D, 256):
                dma('pool', wbf_in[r:r + 256, 0:2048], w_in[r:r + 256, 0:2048], pwrites=[B('wbf_inC')])
            convert(wbf_uq, w_uq, 512, 1536, 'wbf_uq', 256)
            src = w_uq.rearrange("r (h c) -> r h c", h=NH)
            dst = wbf_uq[:, 1536:2048].rearrange("r (h c) -> r h c", h=NH)
            dma('pool', dst[:, :, 0:32], src[:, :, 160:192], pwrites=[B('wbf_uq')])
            dma('pool', dst[:, :, 32:64], src[:, :, 128:160], pwrites=[B('wbf_uq')])
            convert(wbf_ukv, w_ukv, 512, 2048, 'wbf_ukv', 256)
            for r in range(0, D, 256):
                bg_chunks.append((wbf_out[r:r + 256, :], w_out[r:r + 256, :], 'wbf_out'))
            for r in range(0, D, 128):
                bg_chunks.append((wbf_gu[r:r + 128, :], w_gu[r:r + 128, :], 'wbf_gu'))
            for r in range(0, DFF, 256):
                bg_chunks.append((wbf_dn[r:r + 256, :], w_dn[r:r + 256, :], 'wbf_dn'))

        def emit_bg(n, clock):
            for _ in range(n):
                if not bg_chunks:
                    return
                dst, src, name = bg_chunks.pop(0)
                dma('pool', dst, src, reads=clock, pwrites=[B(name)])

        def phase0(vr):
            A = Arena(arena)
            A.off = G_END + 128 * 6 * 4 + 64
            sT = A.alloc(BF16, [128, 16, 2])
            wblk = [A.alloc(BF16, [128, 16, 512]) for _ in range(2)]
            wf = [A.alloc(F32, [128, 16, 512]) for _ in range(3)]
            brow = [A.alloc(F32, [128, 512]) for _ in range(4)]
            grow = [A.alloc(F32, [128, 512]) for _ in range(4)]
            mrow = [A.alloc(F32, [128, 512]) for _ in range(2)]
            for g in range(2):
                act(sT[:, :, g], vecT[:, 5, g * 16:(g + 1) * 16], AF.Silu, reads=[B('vecT')], pwrites=[B('sT')])
            gvecs = {1: g_pre_mix, 2: g_post_mix, 4: g_pre_ffn, 5: g_post_ffn}
            def loads(b):
                sl = b % 3
                k2 = b % 2
                sec = b // 4
                c0 = (b % 4) * 512
                dma('sp' if b % 2 == 0 else 'act', wf[sl],
                    w_ada[:, b * 512:(b + 1) * 512].rearrange("(c p) n -> p c n", p=128), writes=[B(f'wf{sl}')])
                k4 = b % 4
                dma('sp', brow[k4][0:2, :], b_ada[b * 512:(b + 1) * 512].partition_broadcast(2),
                    writes=[B(f'brow{k4}')])
                if sec in gvecs:
                    dma('sp', grow[k4][0:2, :], gvecs[sec][c0:c0 + 512].partition_broadcast(2),
                        writes=[B(f'grow{k4}')])
            loads(0)
            loads(1)
            for b in range(24):
                sl = b % 3
                k2 = b % 2
                sec = b // 4
                c0 = (b % 4) * 512
                dve(lambda e, sl=sl, k2=k2: e.tensor_copy(out=wblk[k2][:, 0:8, :], in_=wf[sl][:, 0:8, :]),
                    reads=[B(f'wf{sl}')], pwrites=[B(f'wblk{k2}')])
                act(wblk[k2][:, 8:16, :], wf[sl][:, 8:16, :], AF.Copy, reads=[B(f'wf{sl}')],
                    pwrites=[B(f'wblk{k2}')])
                ps = bank(k2)[0:2, :]
                mm(ps, [(sT[:, c, :], wblk[k2][:, c, :]) for c in range(16)], [B('sT'), B(f'wblk{k2}')], PSB[k2])
                m = mrow[k2][0:2, :]
                mb = B(f'mrow{k2}')
                k4 = b % 4
                br = brow[k4][0:2, :]
                gr = grow[k4][0:2, :]
                if sec in (1, 4):
                    dve(lambda e, m=m, ps=ps, br=br: e.scalar_tensor_tensor(out=m, in0=ps, scalar=1.0, in1=br,
                                                                             op0=ALU.add, op1=ALU.add),
                        reads=[PSB[k2], B(f'brow{k4}')], writes=[mb])
                else:
                    dve(lambda e, m=m, ps=ps, br=br: e.tensor_tensor(out=m, in0=ps, in1=br, op=ALU.add),
                        reads=[PSB[k2], B(f'brow{k4}')], writes=[mb])
                if sec in gvecs:
                    dve(lambda e, m=m, gr=gr: e.tensor_tensor(out=m, in0=m, in1=gr, op=ALU.mult),
                        reads=[B(f'grow{k4}')], writes=[mb])
                dma('pool', modscr[:, sec, c0:c0 + 512], m, reads=[mb], pwrites=[B('modscr')], owner=mb)
                if b + 2 < 24:
                    loads(b + 2)
            for g in range(2):
                dma('sp', vr[0:96, g, :], modscr[g].rearrange("s (c p) -> (s c) p", p=128), reads=[B('modscr')],
                    writes=[B(f'vr{g}')])
                mm(bank(2 + g)[:, 0:96], [(vr[0:96, g, :], identf[0:96, 0:96])], [B(f'vr{g}'), B('identf')],
                   PSB[2 + g])
                dve(lambda e, g=g: e.tensor_copy(out=vecT[:, g, 0:96], in_=bank(2 + g)[:, 0:96]),
                    reads=[PSB[2 + g]], pwrites=[B('vecT')])

        class NormT:
            def __init__(self, A, nx=3):
                self.X = [A.alloc(F32, [128, D]) for _ in range(nx)]
                self.junk = A.alloc(BF16, [128, D])
                self.xn = [A.alloc(BF16, [128, D]) for _ in range(3)]
                self.i = 0
                self.pending = None

            def stats_rstd(self, xs, xb, n, col):
                sc = ssx[0:n, col:col + 1]
                sb = B(f'ssx{col}')
                act(self.junk[0:n, :], xs[0:n, :], AF.Square, reads=[xb], writes=[sb], accum_out=sc)
                act(sc, sc, AF.Sqrt, reads=[], writes=[sb], scale=1.0 / D, bias=EPS)
                dve(lambda e: e.reciprocal(out=sc, in_=sc), writes=[sb])
                return sc, sb

            def norm_T(self, xs, xb, n, hTv, hTb, col0, GT_, SHT_, pti):
                i = self.i
                self.i += 1
                sc, sb = self.stats_rstd(xs, xb, n, i % 4)
                xn = self.xn[i % 3]
                xnb = B(f'xn{i % 3}')
                dve(lambda e: e.tensor_scalar(out=xn[0:n, :], in0=xs[0:n, :], scalar1=sc, scalar2=None,
                                              op0=ALU.mult), reads=[xb, sb], writes=[xnb])
                prev = self.pending
                self.pending = (xn, xnb, n, hTv, hTb, col0, GT_, SHT_, pti)
                if prev is not None:
                    self._stage2(*prev)

            def flush(self):
                if self.pending is not None:
                    prev = self.pending
                    self.pending = None
                    self._stage2(*prev)

            def _stage2(self, xn, xnb, n, hTv, hTb, col0, GT_, SHT_, pti):
                pt = ptview(pti)
                for hb_ in range(2):
                    pb = PSB[2 * pti + hb_]

                    def tr(e, hb_=hb_):
                        inst = None
                        for c in range(8 * hb_, 8 * hb_ + 8):
                            inst = e.transpose(out=pt[:, c, 0:n], in_=xn[0:n, c * 128:(c + 1) * 128],
                                               identity=identb[0:n, 0:n])
                        return inst
                    P.op('pe', tr, reads=[xnb, B('identb')], writes=[pb])
                for c in (0, 1, 2, 3, 4):
                    act(hTv[:, c, col0:col0 + n], pt[:, c, 0:n], AF.Identity, reads=[PSB[2 * pti], B('vecT')],
                        pwrites=[hTb], scale=GT_(c), bias=SHT_(c))
                for c in (8, 9, 10, 11, 12, 13, 14, 15, 5, 6, 7):
                    o = hTv[:, c, col0:col0 + n]
                    src = pt[:, c, 0:n]
                    pb = PSB[2 * pti + c // 8]
                    dve(lambda e, o=o, src=src, c=c: e.tensor_scalar(out=o, in0=src, scalar1=GT_(c),
                                                                     scalar2=SHT_(c), op0=ALU.mult, op1=ALU.add),
                        reads=[pb, B('vecT')], pwrites=[hTb])

        def phaseA1(g):
            A = Arena(arena)
            winA = A.alloc(BF16, [128, 16, 1152])
            NT = NormT(A)
            hT = [A.alloc(BF16, [128, 16, 512]) for _ in range(3)]
            lat32 = [A.alloc(F32, [128, 4, 512]) for _ in range(2)]
            sq = [A.alloc(BF16, [128, 4, 512]) for _ in range(2)]
            rst = [A.alloc(F32, [128, 512]) for _ in range(2)]
            nst = [A.alloc(BF16, [128, 4, 512]) for _ in range(2)]
            ctab = [A.alloc(F32, [128, 512]) for _ in range(2)]
            stab = [A.alloc(F32, [128, 512]) for _ in range(2)]
            r1 = A.alloc(F32, [128, 512])
            r2 = A.alloc(F32, [128, 512])
            krs = A.alloc(BF16, [128, 512])
            dma('sp', winA, wbf_in[:, 2048:3200].rearrange("(c p) n -> p c n", p=128), reads=[B('wbf_inA')],
                writes=[B('winA')])
            nt = ctx_tiles[g]
            st = {'lat': 0, 'mb': 0}

            def transposes(t):
                hs = t % 3
                for s in range(4):
                    xi = NT.i
                    xs = NT.X[xi % 3]
                    xb = B(f'X{xi % 3}')
                    r0 = (t * 4 + s) * 128
                    dma('sp', xs, x_d[g][r0:r0 + 128, :], writes=[xb])
                    NT.norm_T(xs, xb, 128, hT[hs], B(f'hT{hs}'), s * 128, lambda c: G1T(g, c),
                              lambda c: SH1T(g, c), xi % 2)

            def latent_a(hs, coff):
                li = st['lat'] % 2
                st['lat'] += 1
                hb = B(f'hT{hs}')
                for c4 in range(4):
                    bi = 4 + st['mb'] % 3
                    st['mb'] += 1
                    mm(bank(bi), [(winA[:, kc, coff + c4 * 128:coff + (c4 + 1) * 128], hT[hs][:, kc, :])
                                  for kc in range(16)], [B('winA'), hb], PSB[bi])
                    dve(lambda e, c4=c4, bi=bi: e.tensor_copy(out=lat32[li][:, c4, :], in_=bank(bi)),
                        reads=[PSB[bi]], pwrites=[B(f'lat32{li}')])
                for c4 in range(4):
                    act(sq[li][:, c4, :], lat32[li][:, c4, :], AF.Square, reads=[B(f'lat32{li}')],
                        pwrites=[B(f'sq{li}')])
                return li

            def latent_b(t, li, gfun, dst, dstb):
                mm(bank(7), [(ones, sq[li][:, c4, :]) for c4 in range(4)], [B('ones'), B(f'sq{li}')], PSB[7])
                act(rst[li], bank(7), AF.Sqrt, reads=[PSB[7]], writes=[B(f'rst{li}')], scale=1.0 / 512, bias=EPS)
                dve(lambda e: e.reciprocal(out=rst[li], in_=rst[li]), writes=[B(f'rst{li}')])
                for c4 in range(4):
                    dve(lambda e, c4=c4: e.scalar_tensor_tensor(out=nst[li][:, c4, :], in0=lat32[li][:, c4, :],
                                                                 scalar=gfun(c4), in1=rst[li], op0=ALU.mult,
                                                                 op1=ALU.mult),
                        reads=[B(f'lat32{li}'), B(f'rst{li}'), B('vecT')], pwrites=[B(f'nst{li}')])
                dma(STQ[g], dst[:, :, t * 512:(t + 1) * 512], nst[li], reads=[B(f'nst{li}')], pwrites=[dstb],
                    owner=B(f'nst{li}'))

            def matmuls(t):
                hs = t % 3
                hb = B(f'hT{hs}')
                if t < own_tiles:
                    dma(STQ[g], h_scr[g][:, :, t * 512:(t + 1) * 512], hT[hs], reads=[hb], pwrites=[B(f'h_scr{g}')],
                        owner=hb)
                li_kv = latent_a(hs, 512)
                ts = t % 2
                dma('sp', ctab[ts][0:64, :], rope_d[g][0, :, t * 512:(t + 1) * 512], writes=[B(f'ctab{ts}')])
                dma('sp', stab[ts][0:64, :], rope_d[g][1, :, t * 512:(t + 1) * 512], writes=[B(f'stab{ts}')])
                ba = 4 + st['mb'] % 3
                st['mb'] += 1
                bb = 4 + st['mb'] % 3
                st['mb'] += 1
                mm(bank(ba)[0:64, :], [(winA[:, kc, 1024:1088], hT[hs][:, kc, :]) for kc in range(16)],
                   [B('winA'), hb], PSB[ba])
                mm(bank(bb)[0:64, :], [(winA[:, kc, 1088:1152], hT[hs][:, kc, :]) for kc in range(16)],
                   [B('winA'), hb], PSB[bb])
                dve(lambda e: e.tensor_tensor(out=r1[0:64, :], in0=bank(ba)[0:64, :], in1=ctab[ts][0:64, :],
                                              op=ALU.mult), reads=[PSB[ba], B(f'ctab{ts}')], writes=[B('r1')])
                dve(lambda e: e.tensor_tensor(out=r2[0:64, :], in0=bank(bb)[0:64, :], in1=stab[ts][0:64, :],
                                              op=ALU.mult), reads=[PSB[bb], B(f'stab{ts}')], writes=[B('r2')])
                dve(lambda e: e.tensor_tensor(out=krs[0:64, :], in0=r1[0:64, :], in1=r2[0:64, :], op=ALU.add),
                    reads=[B('r1'), B('r2')], writes=[B('krs')])
                dma(STQ[g], kr_scr[g][:, t * 512:(t + 1) * 512], krs[0:64, :], reads=[B('krs')],
                    pwrites=[B(f'kr_scr{g}')], owner=B('krs'))
                li_q = latent_a(hs, 0) if t < own_tiles else None
                latent_b(t, li_kv, GKV, kvn_scr[g], B(f'kvn_scr{g}'))
                if li_q is not None:
                    latent_b(t, li_q, GQ, qn_scr[g], B(f'qn_scr{g}'))

            transposes(0)
            if nt > 1:
                transposes(1)
            for t in range(nt):
                if t + 2 < nt:
                    transposes(t + 2)
                else:
                    NT.flush()
                matmuls(t)
            hs = nt % 3
            xi = NT.i
            xs = NT.X[xi % 3]
            xb = B(f'X{xi % 3}')
            dma('sp', xs[0:32, :], halo_d[g], writes=[xb])
            NT.norm_T(xs, xb, 32, hT[hs], B(f'hT{hs}'), 0, lambda c: G1T(g, c), lambda c: SH1T(g, c), xi % 2)
            NT.flush()
            dma(STQ[g], h_scr[g][:, :, NOWN:NOWN + 32], hT[hs][:, :, 0:32], reads=[B(f'hT{hs}')],
                pwrites=[B(f'h_scr{g}')], owner=B(f'hT{hs}'))

        HGW = NOWN + 32

        def phaseA2a(g):
            A = Arena(arena)
            hg = A.alloc(BF16, [128, 8, HGW])
            winC = A.alloc(BF16, [128, 16, 2048])
            hTl = [A.alloc(BF16, [128, 16, 512]) for _ in range(2)]
            sg = [A.alloc(F32, [128, 512]) for _ in range(2)]
            htmp = A.alloc(F32, [128, 32])
            for half in range(2):
                dma('sp', winC[:, :, half * 1024:(half + 1) * 1024],
                    wbf_in[:, half * 1024:(half + 1) * 1024].rearrange("(c p) n -> p c n", p=128),
                    reads=[B('wbf_inC')], pwrites=[B('winC')])
            k = 0
            for t in range(own_tiles + 1):
                hs = t % 2
                hb = B(f'hTl{hs}')
                halo = (t == own_tiles)
                n = 32 if halo else 512
                c0 = NOWN if halo else t * 512
                dma('sp', hTl[hs][:, :, 0:n], h_scr[g][:, :, c0:c0 + n], reads=[B(f'h_scr{g}')], writes=[hb])
                for j in range(8):
                    ba = (2 * k) % 8
                    bg = (2 * k + 1) % 8
                    si = k % 2
                    k += 1
                    mm(bank(ba)[:, 0:n], [(winC[:, kc, j * 128:(j + 1) * 128], hTl[hs][:, kc, 0:n])
                                          for kc in range(16)], [B('winC'), hb], PSB[ba])
                    mm(bank(bg)[:, 0:n], [(winC[:, kc, 1024 + j * 128:1024 + (j + 1) * 128], hTl[hs][:, kc, 0:n])
                                          for kc in range(16)], [B('winC'), hb], PSB[bg])
                    act(sg[si][:, 0:n], bank(bg)[:, 0:n], AF.Sigmoid, reads=[PSB[bg]], writes=[B(f'sg{si}')])
                    if not halo:
                        dve(lambda e, j=j, ba=ba, si=si, t=t: e.tensor_tensor(
                            out=hg[:, j, 15 + t * 512:15 + (t + 1) * 512], in0=bank(ba), in1=sg[si], op=ALU.mult),
                            reads=[PSB[ba], B(f'sg{si}')], pwrites=[B('hg')])
                    else:
                        dve(lambda e, ba=ba, si=si: e.tensor_tensor(out=htmp, in0=bank(ba)[:, 0:32],
                                                                     in1=sg[si][:, 0:32], op=ALU.mult),
                            reads=[PSB[ba], B(f'sg{si}')], writes=[B('htmp')])
                        dve(lambda e: e.tensor_tensor(out=htmp, in0=htmp, in1=hmask[:, g, :], op=ALU.mult),
                            reads=[B('hmask')], writes=[B('htmp')])
                        dve(lambda e, j=j: e.tensor_copy(out=hg[:, j, 0:15], in_=htmp[:, 0:15]),
                            reads=[B('htmp')], pwrites=[B('hg')])
                        dve(lambda e, j=j: e.tensor_copy(out=hg[:, j, 15 + NOWN:30 + NOWN], in_=htmp[:, 15:30]),
                            reads=[B('htmp')], pwrites=[B('hg')])

        def phaseA2b(g):
            A = Arena(arena)
            hg = A.alloc(BF16, [128, 8, HGW])
            dg = A.alloc(BF16, [128, 248, 128])
            y32 = A.alloc(F32, [128, 8, 512])
            ysq = A.alloc(BF16, [128, 8, 512])
            ybf = A.alloc(BF16, [128, 8, 512])
            mean = A.alloc(F32, [128, 512])
            var = A.alloc(F32, [128, 512])
            rstd = A.alloc(F32, [128, 512])
            co = [A.alloc(BF16, [128, 8, 512]) for _ in range(2)]
            for idx in range(248):
                k, c = idx // 8, idx % 8
                if idx % 2 == 0:
                    dve(lambda e, idx=idx, k=k, c=c: e.tensor_scalar(out=dg[:, idx, :], in0=identb,
                                                                      scalar1=WDW(k, c), scalar2=None, op0=ALU.mult),
                        reads=[B('identb'), B('vecT')], pwrites=[B('dg')])
                else:
                    act(dg[:, idx, :], identb, AF.Copy, reads=[B('identb'), B('vecT')], pwrites=[B('dg')],
                        scale=WDW(k, c))
            for t in range(own_tiles):
                for c in range(8):
                    bi = c % 4
                    mm(bank(bi), [(dg[:, k * 8 + c, :], hg[:, c, t * 512 + k:t * 512 + k + 512]) for k in range(31)],
                       [B('dg'), B('hg')], PSB[bi])
                    act(y32[:, c, :], bank(bi), AF.Identity, reads=[PSB[bi], B('vecT')], pwrites=[B('y32')],
                        bias=BDW(c))
                    act(ysq[:, c, :], bank(bi), AF.Square, reads=[PSB[bi], B('vecT')], pwrites=[B('ysq')],
                        bias=BDW(c))
                    dve(lambda e, c=c: e.tensor_copy(out=ybf[:, c, :], in_=y32[:, c, :]), reads=[B('y32')],
                        pwrites=[B('ybf')])
                mm(bank(4), [(ones, ybf[:, c, :]) for c in range(8)], [B('ones'), B('ybf')], PSB[4])
                mm(bank(5), [(ones, ysq[:, c, :]) for c in range(8)], [B('ones'), B('ysq')], PSB[5])
                dve(lambda e: e.tensor_scalar(out=mean, in0=bank(4), scalar1=1.0 / DC, scalar2=None, op0=ALU.mult),
                    reads=[PSB[4]], writes=[B('mean')])
                dve(lambda e: e.tensor_tensor(out=var, in0=mean, in1=mean, op=ALU.mult), reads=[B('mean')],
                    writes=[B('var')])
                dve(lambda e: e.scalar_tensor_tensor(out=var, in0=bank(5), scalar=1.0 / DC, in1=var, op0=ALU.mult,
                                                     op1=ALU.subtract), reads=[PSB[5]], writes=[B('var')])
                act(rstd, var, AF.Sqrt, reads=[B('var')], writes=[B('rstd')], bias=EPS)
                dve(lambda e: e.reciprocal(out=rstd, in_=rstd), writes=[B('rstd')])
                cs = t % 2
                for c in range(8):
                    dve(lambda e, c=c: e.tensor_tensor(out=y32[:, c, :], in0=y32[:, c, :], in1=mean,
                                                       op=ALU.subtract), reads=[B('mean')], writes=[B('y32')])
                    dve(lambda e, c=c: e.tensor_tensor(out=y32[:, c, :], in0=y32[:, c, :], in1=rstd, op=ALU.mult),
                        reads=[B('rstd')], writes=[B('y32')])
                    act(co[cs][:, c, :], y32[:, c, :], AF.Silu, reads=[B('y32'), B('vecT')], pwrites=[B(f'co{cs}')],
                        scale=GCV(c), bias=BCV(c))
                dma(STQ[g], conv_scr[g][:, :, t * 512:(t + 1) * 512], co[cs], reads=[B(f'co{cs}')],
                    pwrites=[B(f'conv_scr{g}')], owner=B(f'co{cs}'))

        def phaseB(g):
            A = Arena(arena)
            nct = ctx_tiles[g]
            nkc = nct * 4
            wq = A.alloc(BF16, [128, 4, 2048])
            wkv = A.alloc(BF16, [128, 4, 2048])
            qn = A.alloc(BF16, [128, 4, NOWN])
            kr = A.alloc(BF16, [128, NCTX[g]])
            cq = A.alloc(F32, [128, NOWN])
            sqt = A.alloc(F32, [128, NOWN])
            Kh = A.alloc(BF16, [128, NCTX[g]])
            Vh = A.alloc(BF16, [128, nkc, 128])
            kvt = [A.alloc(BF16, [128, 4, 512]) for _ in range(3)]
            qnope = [A.alloc(BF16, [128, 512]) for _ in range(2)]
            qrope = [A.alloc(BF16, [128, 512]) for _ in range(2)]
            pT = [A.alloc(BF16, [128, 512]) for _ in range(4)]
            ptmp = A.alloc(BF16, [128, 512])
            r1 = A.alloc(F32, [128, 512])
            r2 = A.alloc(F32, [128, 512])
            rs = A.alloc(F32, [128, 512])
            ao = [A.alloc(BF16, [128, 512]) for _ in range(2)]
            acc = [A.alloc(F32, [128, 512]) for _ in range(2)]
            shi = A.alloc(BF16, [128, 512])
            slo = A.alloc(BF16, [128, 512])
            dma('sp', wq, wbf_uq.rearrange("(c p) n -> p c n", p=128), reads=[B('wbf_uq')], writes=[B('wq')])
            dma('sp', wkv, wbf_ukv.rearrange("(c p) n -> p c n", p=128), reads=[B('wbf_ukv')], writes=[B('wkv')])
            dma('sp', qn, qn_scr[g], reads=[B(f'qn_scr{g}')], writes=[B('qn')])
            dma('sp', kr[0:64, :], kr_scr[g], reads=[B(f'kr_scr{g}')], writes=[B('kr')])
            dma('sp', cq[0:64, :], rope_d[g][0, :, 0:NOWN], writes=[B('cq')])
            dma('sp', sqt[0:64, :], rope_d[g][1, :, 0:NOWN], writes=[B('sqt')])
            kvi = 0
            pti = 0
            qi = 0
            sti = 0
            for h in range(NH):
                if g == 0:
                    emit_bg((len(bg_chunks) + (NH - h) - 1) // (NH - h), [B('Kh')])
                for t in range(nct):
                    ks = kvi % 3
                    kvi += 1
                    kb = B(f'kvt{ks}')
                    dma('sp', kvt[ks], kvn_scr[g][:, :, t * 512:(t + 1) * 512], reads=[B(f'kvn_scr{g}')],
                        writes=[kb])
                    mm(bank(6), [(wkv[:, c4, h * 256:h * 256 + 128], kvt[ks][:, c4, :]) for c4 in range(4)],
                       [B('wkv'), kb], PSB[6])
                    act(Kh[:, t * 512:(t + 1) * 512], bank(6), AF.Copy, reads=[PSB[6]], pwrites=[B('Kh')])
                    vb = bank(7).rearrange("p (s d) -> p s d", s=4)

                    def vfn(e, ks=ks, vb=vb, h=h):
                        inst = None
                        for s in range(4):
                            for c4 in range(4):
                                inst = e.matmul(vb[:, s, :], lhsT=kvt[ks][:, c4, s * 128:(s + 1) * 128],
                                                rhs=wkv[:, c4, h * 256 + 128:h * 256 + 256],
                                                start=(c4 == 0), stop=(c4 == 3))
                        return inst
                    P.op('pe', vfn, reads=[B('wkv'), kb], writes=[PSB[7]])
                    dve(lambda e, t=t, vb=vb: e.tensor_copy(out=Vh[:, t * 4:(t + 1) * 4, :], in_=vb),
                        reads=[PSB[7]], pwrites=[B('Vh')])
                for j in range(own_tiles):
                    qs = qi % 2
                    qi += 1
                    jsl = slice(j * 512, (j + 1) * 512)
                    mm(bank(6), [(wq[:, c4, h * 192:h * 192 + 128], qn[:, c4, jsl]) for c4 in range(4)],
                       [B('wq'), B('qn')], PSB[6])
                    act(qnope[qs], bank(6), AF.Copy, reads=[PSB[6]], writes=[B(f'qnope{qs}')])
                    mm(bank(7)[0:64, :], [(wq[:, c4, h * 192 + 128:h * 192 + 192], qn[:, c4, jsl]) for c4 in range(4)],
                       [B('wq'), B('qn')], PSB[7])
                    mm(bank(6)[0:64, :], [(wq[:, c4, 1536 + h * 64:1536 + (h + 1) * 64], qn[:, c4, jsl])
                                          for c4 in range(4)], [B('wq'), B('qn')], PSB[6])
                    dve(lambda e, jsl=jsl: e.tensor_tensor(out=r1[0:64, :], in0=bank(7)[0:64, :], in1=cq[0:64, jsl],
                                                           op=ALU.mult), reads=[PSB[7], B('cq')], writes=[B('r1')])
                    dve(lambda e, jsl=jsl: e.tensor_tensor(out=r2[0:64, :], in0=bank(6)[0:64, :], in1=sqt[0:64, jsl],
                                                           op=ALU.mult), reads=[PSB[6], B('sqt')], writes=[B('r2')])
                    dve(lambda e, qs=qs: e.tensor_tensor(out=qrope[qs][0:64, :], in0=r1[0:64, :], in1=r2[0:64, :],
                                                         op=ALU.add), reads=[B('r1'), B('r2')],
                        writes=[B(f'qrope{qs}')])
                    ob = 3 + (qi % 2)
                    sb_ = 5
                    ac = acc[qi % 2]
                    acb = B(f'acc{qi % 2}')
                    qdeps = [B('Kh'), B('kr'), B(f'qnope{qs}'), B(f'qrope{qs}')]

                    def st_op(kc):
                        nonlocal sti
                        stb = sti % 3
                        sti += 1
                        ksl = slice(kc * 128, (kc + 1) * 128)
                        mm(bank(stb), [(Kh[:, ksl], qnope[qs]), (kr[0:64, ksl], qrope[qs][0:64, :])], qdeps, PSB[stb])
                        return stb
                    pprev = 0
                    stq = [st_op(0)]
                    if nkc > 1:
                        stq.append(st_op(1))
                    for kc in range(nkc):
                        stb = stq.pop(0)
                        if kc + 2 < nkc:
                            stq.append(st_op(kc + 2))
                        ps_ = pti % 4
                        pti += 1
                        act(pT[ps_], bank(stb), AF.Exp, reads=[PSB[stb]], writes=[B(f'pT{ps_}')], scale=SCALE)

                        def pv(e, kc=kc, ps_=ps_, ob=ob, sb_=sb_):
                            e.matmul(bank(ob), lhsT=Vh[:, kc, :], rhs=pT[ps_], start=(kc == 0), stop=(kc == nkc - 1))
                            return e.matmul(bank(sb_), lhsT=ones, rhs=pT[ps_], start=(kc == 0),
                                            stop=(kc == nkc - 1))
                        if kc == 0:
                            P.op('pe', pv, reads=[B('Vh'), B('ones'), B(f'pT{ps_}')], writes=[PSB[ob], PSB[sb_]])
                        else:
                            P.op('pe', pv, reads=[B('Vh'), B('ones'), B(f'pT{ps_}')], pwrites=[PSB[ob], PSB[sb_]])
                    dve(lambda e, sb_=sb_: e.reciprocal(out=rs, in_=bank(sb_)), reads=[PSB[sb_]], writes=[B('rs')])
                    a_ = qi % 2
                    dve(lambda e, ob=ob, a_=a_: e.tensor_tensor(out=ao[a_], in0=bank(ob), in1=rs, op=ALU.mult),
                        reads=[PSB[ob], B('rs')], writes=[B(f'ao{a_}')])
                    dma(STQ[g], attn_scr[g][:, h, jsl], ao[a_], reads=[B(f'ao{a_}')], pwrites=[B(f'attn_scr{g}')],
                        owner=B(f'ao{a_}'))

        def phaseC1(g):
            A = Arena(arena)
            wo = A.alloc(BF16, [128, 16, 2048])
            NT = NormT(A, nx=4)
            mixT = [A.alloc(BF16, [128, 16, 512]) for _ in range(2)]
            mo = [A.alloc(F32, [128, D]) for _ in range(3)]
            gt1 = A.alloc(F32, [128, D])
            h2T = [A.alloc(BF16, [128, 16, 512]) for _ in range(1)]
            for half in range(2):
                dma('pool', wo[:, :, half * 1024:(half + 1) * 1024],
                    wbf_out[:, half * 1024:(half + 1) * 1024].rearrange("(c p) n -> p c n", p=128),
                    reads=[B('wbf_out')], pwrites=[B('wo')])
            dma('sp', gt1, modscr[g, 2, :].partition_broadcast(128), reads=[B('modscr')], writes=[B('gt1')])
            nsub = own_tiles * 4

            def mo_stage(i):
                t, s_ = i // 4, i % 4
                ms = t % 2
                mb = B(f'mixT{ms}')
                if s_ == 0:
                    tsl = slice(t * 512, (t + 1) * 512)
                    dma('sp', mixT[ms][:, 0:8, :], conv_scr[g][:, :, tsl], reads=[B(f'conv_scr{g}')], pwrites=[mb])
                    dma('sp', mixT[ms][:, 8:16, :], attn_scr[g][:, :, tsl], reads=[B(f'attn_scr{g}')], pwrites=[mb])
                xs = NT.X[i % 4]
                xb = B(f'X{i % 4}')
                r0 = i * 128
                dma('sp', xs, x_d[g][r0:r0 + 128, :], writes=[xb])
                m = mo[i % 3]
                mob = B(f'mo{i % 3}')
                for cb in range(4):
                    bi = 4 + cb
                    mm(bank(bi), [(mixT[ms][:, kc, s_ * 128:(s_ + 1) * 128], wo[:, kc, cb * 512:(cb + 1) * 512])
                                  for kc in range(16)], [mb, B('wo')], PSB[bi])
                    if cb % 2 == 0:
                        act(m[:, cb * 512:(cb + 1) * 512], bank(bi), AF.Copy, reads=[PSB[bi]], pwrites=[mob])
                    else:
                        dve(lambda e, m=m, cb=cb, bi=bi: e.tensor_copy(out=m[:, cb * 512:(cb + 1) * 512],
                                                                       in_=bank(bi)),
                            reads=[PSB[bi]], pwrites=[mob])

            def post_stage(i):
                t, s_ = i // 4, i % 4
                hs = 0
                xs = NT.X[i % 4]
                xb = B(f'X{i % 4}')
                r0 = i * 128
                m = mo[i % 3]
                mob = B(f'mo{i % 3}')
                sc, sb = NT.stats_rstd(m, mob, 128, 4 + i % 2)
                dve(lambda e, m=m, sc=sc: e.scalar_tensor_tensor(out=m, in0=m, scalar=sc, in1=gt1, op0=ALU.mult,
                                                                  op1=ALU.mult),
                    reads=[sb, B('gt1')], writes=[mob])
                dve(lambda e, m=m, xs=xs: e.tensor_tensor(out=xs, in0=xs, in1=m, op=ALU.add), reads=[mob],
                    writes=[xb])
                dma(STQ[g], x1_scr[g][r0:r0 + 128, :], xs, reads=[xb], pwrites=[B(f'x1_scr{g}')], owner=xb)
                NT.norm_T(xs, xb, 128, h2T[hs], B(f'h2T{hs}'), s_ * 128, lambda c: G2T(g, c),
                          lambda c: SH2T(g, c), i % 2)
                if i >= 1 and (i - 1) % 4 == 3:
                    store_h2((i - 1) // 4)

            def store_h2(t):
                tsl = slice(t * 512, (t + 1) * 512)
                dma(STQ[g], h2_scr[g][:, :, tsl], h2T[0], reads=[B('h2T0')], pwrites=[B(f'h2_scr{g}')], owner=B('h2T0'))

            mo_stage(0)
            if nsub > 1:
                mo_stage(1)
            for i in range(nsub):
                if i + 2 < nsub:
                    mo_stage(i + 2)
                post_stage(i)
            NT.flush()
            store_h2(own_tiles - 1)

        def phaseC2(g):
            A = Arena(arena)
            h2T = [A.alloc(BF16, [128, 16, 512]) for _ in range(2)]
            actT = A.alloc(BF16, [128, NJ, 512])
            wgu = [A.alloc(BF16, [128, 16, 2, 256]) for _ in range(2)]
            wdn = [A.alloc(BF16, [128, 11, 512]) for _ in range(2)]
            f32 = [A.alloc(F32, [128, D]) for _ in range(4)]
            X1 = [A.alloc(F32, [128, D]) for _ in range(2)]
            gt2 = A.alloc(F32, [128, D])
            junk = A.alloc(BF16, [128, D])
            sg = [A.alloc(F32, [128, 512]) for _ in range(2)]
            dma('sp', gt2, modscr[g, 5, :].partition_broadcast(128), reads=[B('modscr')], writes=[B('gt2')])
            gi = 0
            di = 0
            ki = 0
            xi = 0
            pending = []

            def epilogue(t, s):
                nonlocal xi
                if True:
                    r0 = t * 512 + s * 128
                    x1 = X1[xi % 2]
                    x1b = B(f'X1{xi % 2}')
                    xi += 1
                    dma('sp', x1, x1_scr[g][r0:r0 + 128, :], reads=[B(f'x1_scr{g}')], writes=[x1b])
                    fb = B(f'f32{s}')
                    f = f32[s]
                    sc = ssx[:, 6 + s % 2:7 + s % 2]
                    sb = B(f'ssx{6 + s % 2}')
                    act(junk, f, AF.Square, reads=[fb], writes=[sb], accum_out=sc)
                    act(sc, sc, AF.Sqrt, writes=[sb], scale=1.0 / D, bias=EPS)
                    dve(lambda e, sc=sc: e.reciprocal(out=sc, in_=sc), writes=[sb])
                    dve(lambda e, f=f, sc=sc: e.scalar_tensor_tensor(out=f, in0=f, scalar=sc, in1=gt2, op0=ALU.mult,
                                                                      op1=ALU.mult),
                        reads=[sb, B('gt2')], writes=[fb])
                    dve(lambda e, f=f, x1=x1: e.tensor_tensor(out=f, in0=f, in1=x1, op=ALU.add), reads=[x1b],
                        writes=[fb])
                    dma('sp', y_d[g][r0:r0 + 128, :], f, reads=[fb], pwrites=[B(f'y{g}')], owner=fb)

            for t in range(own_tiles):
                hb = B(f'h2Tl{t % 2}')
                h2 = h2T[t % 2]
                if t == 0:
                    dma('sp', h2, h2_scr[g][:, :, 0:512], reads=[B(f'h2_scr{g}')], writes=[hb])
                for jb in range(NJ // 2):
                    if pending and jb in (3, 7, 11, 15):
                        epilogue(*pending.pop(0))
                    ws = gi % 2
                    gi += 1
                    wb = B(f'wgu{ws}')
                    for u in range(2):
                        c0 = u * DFF + jb * 256
                        dma('pool', wgu[ws][:, :, u, :], wbf_gu[:, c0:c0 + 256].rearrange("(c p) n -> p c n", p=128),
                            reads=[B('wbf_gu')], pwrites=[wb])
                    for jj in range(2):
                        j = jb * 2 + jj
                        bg = (2 * ki) % 4
                        bu = bg + 1
                        si = ki % 2
                        ki += 1
                        mm(bank(bg), [(wgu[ws][:, kc, 0, jj * 128:(jj + 1) * 128], h2[:, kc, :])
                                      for kc in range(16)], [wb, hb], PSB[bg])
                        mm(bank(bu), [(wgu[ws][:, kc, 1, jj * 128:(jj + 1) * 128], h2[:, kc, :])
                                      for kc in range(16)], [wb, hb], PSB[bu])
                        act(sg[si], bank(bg), AF.Silu, reads=[PSB[bg]], writes=[B(f'sg{si}')])
                        dve(lambda e, j=j, bu=bu, si=si: e.tensor_tensor(out=actT[:, j, :], in0=bank(bu), in1=sg[si],
                                                                         op=ALU.mult),
                            reads=[PSB[bu], B(f'sg{si}')], pwrites=[B('actT')])
                if t + 1 < own_tiles:
                    dma('sp', h2T[(t + 1) % 2], h2_scr[g][:, :, (t + 1) * 512:(t + 2) * 512],
                        reads=[B(f'h2_scr{g}')], writes=[B(f'h2Tl{(t + 1) % 2}')])
                for cb in range(4):
                    for kg in range(4):
                        ws = di % 2
                        di += 1
                        wb = B(f'wdn{ws}')
                        dma('pool', wdn[ws], wbf_dn[kg * 1408:(kg + 1) * 1408, cb * 512:(cb + 1) * 512].rearrange(
                            "(c p) n -> p c n", p=128), reads=[B('wbf_dn')], writes=[wb])

                        def dfn(e, ws=ws, kg=kg):
                            inst = None
                            for s in range(4):
                                for kc in range(11):
                                    inst = e.matmul(bank(4 + s), lhsT=actT[:, kg * 11 + kc, s * 128:(s + 1) * 128],
                                                    rhs=wdn[ws][:, kc, :], start=(kg == 0 and kc == 0),
                                                    stop=(kg == 3 and kc == 10))
                            return inst
                        if kg == 0:
                            P.op('pe', dfn, reads=[B('actT'), wb], writes=[PSB[4 + s] for s in range(4)])
                        else:
                            P.op('pe', dfn, reads=[B('actT'), wb], pwrites=[PSB[4 + s] for s in range(4)])
                    for s in range(4):
                        o = f32[s][:, cb * 512:(cb + 1) * 512]
                        if s % 2 == 0:
                            act(o, bank(4 + s), AF.Copy, reads=[PSB[4 + s]], pwrites=[B(f'f32{s}')])
                        else:
                            dve(lambda e, o=o, s=s: e.tensor_copy(out=o, in_=bank(4 + s)), reads=[PSB[4 + s]],
                                pwrites=[B(f'f32{s}')])
                for s in range(4):
                    pending.append((t, s))
            while pending:
                epilogue(*pending.pop(0))

        vr = setup()
        if 'cs' in phases:
            convert_pre()
        if '0' in phases:
            phase0(vr)
        P.barrier()
        if 'cb' in phases:
            convert_bg_early()
        for g in range(2):
            if 'A1' in phases:
                phaseA1(g)
                P.barrier()
            if 'A2' in phases:
                phaseA2a(g)
                P.barrier()
                phaseA2b(g)
                P.barrier()
            if 'B' in phases:
                phaseB(g)
                P.barrier()
            if 'C1' in phases:
                emit_bg(len(bg_chunks), [])
                phaseC1(g)
                P.barrier()
            if 'C2' in phases:
                phaseC2(g)
                P.barrier()

        names = P.sem_names()
        assert len(names) <= 140, len(names)
        sems = {n: es.enter_context(nc.semaphore(n)) for n in names}
        with nc.Block() as block:
            @block.tensor
            def _(e):
                P.replay('pe', e, sems)

            @block.scalar
            def _(e):
                P.replay('act', e, sems)

            @block.vector
            def _(e):
                P.replay('dve', e, sems)

            @block.gpsimd
            def _(e):
                P.replay('pool', e, sems)

            @block.sync
            def _(e):
                P.replay('sp', e, sems)
    nc._prog_stats = {e: len(P.ops[e]) for e in ENGS}
    nc._nsems = len(names)
    return nc


def _rope_tables(pos):
    inv = (1.0 / (10000.0 ** (np.arange(0, 64, 2, dtype=np.float32) / np.float32(64)))).astype(np.float32)
    ang = pos.astype(np.float32)[:, None] * inv[None, :]
    cos = np.cos(ang).astype(np.float32).T
    sin = np.sin(ang).astype(np.float32).T
    return np.ascontiguousarray(np.stack([np.concatenate([cos, cos], 0), np.concatenate([-sin, sin], 0)], 0))


def make_in_maps(inputs, own_tiles=4, ctx_tiles=(4, 16)):
    NOWN = own_tiles * 512
    xp = np.asarray(inputs['x_prompt'], np.float32)
    xs = np.asarray(inputs['x_sample'], np.float32)
    cp = np.asarray(inputs['c_prompt'], np.float32)
    cs = np.asarray(inputs['c_sample'], np.float32)
    wnames = ['w_ada', 'b_ada', 'g_pre_mix', 'g_post_mix', 'w_in', 'w_dw', 'b_dw', 'g_conv', 'b_conv', 'g_q_lat',
              'w_uq', 'g_kv_lat', 'w_ukv', 'w_out', 'g_pre_ffn', 'g_post_ffn', 'w_gate_up', 'w_down']
    shared = {n: np.ascontiguousarray(np.asarray(inputs[n], np.float32)[0]) for n in wnames}
    shared['ident'] = np.eye(128, dtype=np.float32)
    nblk = 8192 // 2048
    in_maps = []
    for c in range(8):
        sqi, qd = c // 4, c % 4
        order = [qd] + [b for b in range(nblk) if b != qd]
        pos = np.concatenate([np.arange(b * 2048, (b + 1) * 2048) for b in order])
        xS = np.concatenate([xs[sqi, b * 2048:(b + 1) * 2048] for b in order], 0)
        nS = ctx_tiles[1] * 512
        nP = ctx_tiles[0] * 512
        if own_tiles < 4:
            o0 = qd * 2048
            own = np.arange(o0, o0 + NOWN)
            rest = np.array([p for p in range(8192) if not (o0 <= p < o0 + NOWN)])
            pos = np.concatenate([own, rest])
            xS = xs[sqi][pos]
        halo = np.zeros((2, 32, D), np.float32)
        hmask = np.zeros((128, 2, 32), np.float32)
        o0 = qd * 2048
        if o0 > 0:
            halo[1, 0:15] = xs[sqi, o0 - 15:o0]
            hmask[:, 1, 0:15] = 1.0
        if o0 + NOWN < 8192:
            halo[1, 15:30] = xs[sqi, o0 + NOWN:o0 + NOWN + 15]
            hmask[:, 1, 15:30] = 1.0
        if NOWN < 2048:
            halo[0, 15:30] = xp[c, NOWN:NOWN + 15]
            hmask[:, 0, 15:30] = 1.0
        m = dict(shared)
        m['xP'] = np.ascontiguousarray(xp[c, :nP])
        m['xS'] = np.ascontiguousarray(xS[:nS])
        m['halo'] = halo
        m['hmask'] = hmask
        m['cpair'] = np.ascontiguousarray(np.stack([cp[c], cs[sqi]], 0))
        m['ropeP'] = np.ascontiguousarray(_rope_tables(np.arange(nP)))
        m['ropeS'] = np.ascontiguousarray(_rope_tables(pos[:nS]))
        in_maps.append(m)
    return in_maps


_NC_CACHE = {}


def kernel(**inputs):
    if 'full' not in _NC_CACHE:
        _NC_CACHE['full'] = build_nc()
    nc = _NC_CACHE['full']
    in_maps = make_in_maps(inputs)
    res = run_bass_kernel_spmd(nc, in_maps, core_ids=list(range(8)))
    y_prompt = np.zeros((8, 2048, D), np.float32)
    y_sample = np.zeros((2, 8192, D), np.float32)
    for c in range(8):
        r = res.results[c]
        y_prompt[c] = r['yP']
        y_sample[c // 4, (c % 4) * 2048:(c % 4 + 1) * 2048] = r['yS']
    return (y_prompt, y_sample)
```
